# Optimizing a Trainium2 kernel written in Bass

```python
import math
import jax
import jax.numpy as jnp
from jax import lax
import numpy as np

D_MODEL = 2048
BATCH = 8
SEQ = 2048
DEPTH = 2

HYENA_WIDTH = D_MODEL // 2
HYENA_ORDER = 2
HYENA_DIRS = 2
SHORT_CONV = 3
FILTER_EMB = 33
FILTER_BANDS = (FILTER_EMB - 1) // 2
FILTER_HIDDEN = 64
FILTER_SIN_W = 1.0
DECAY_TARGET = 1e-2
FAST_DECAY_PCT = 0.3
SLOW_DECAY_PCT = 1.5
MIN_DECAY = math.log(DECAY_TARGET) / FAST_DECAY_PCT
MAX_DECAY = math.log(DECAY_TARGET) / SLOW_DECAY_PCT

HEAD_DIM = 128
N_HEADS = (D_MODEL // 2) // HEAD_DIM
N_KV_HEADS = 2
GROUP = N_HEADS // N_KV_HEADS
ATTN_WIDTH = N_HEADS * HEAD_DIM
KV_WIDTH = N_KV_HEADS * HEAD_DIM
WINDOW = 128
BLOCK = 128
ROPE_THETA = 500000.0
ROPE_DIM = HEAD_DIM // 4
EPS = 1e-6

IN_SIZES = ((HYENA_ORDER + 1) * HYENA_WIDTH, HYENA_WIDTH, ATTN_WIDTH, KV_WIDTH, KV_WIDTH, ATTN_WIDTH, D_MODEL, D_MODEL)
IN_WIDTH = sum(IN_SIZES)

kernel_name = "hyena_swa_gated_hybrid_encoder"


def rms_norm(x, g):
    xf = x.astype(jnp.float32)
    y = xf * lax.rsqrt(jnp.mean(xf * xf, axis=-1, keepdims=True) + EPS)
    return (y * g.astype(jnp.float32)).astype(x.dtype)


def short_conv_centred(u, w, b):
    L = u.shape[1]
    p = SHORT_CONV // 2
    up = jnp.pad(u, ((0, 0), (p, SHORT_CONV - 1 - p), (0, 0)))
    out = b
    for j in range(SHORT_CONV):
        out = out + up[:, j:j + L] * w[j]
    return out


def hyena_kernels(L, w1, b1, w2, b2, w3, b3, w4, freq):
    f32 = jnp.float32
    t = jnp.linspace(0.0, 1.0, L, dtype=f32)[:, None]
    bands = jnp.linspace(1e-4, FILTER_BANDS - 1, FILTER_BANDS, dtype=f32)[None, :]
    ang = (2.0 * math.pi / L) * jnp.arange(L, dtype=f32)[:, None] * bands
    feats = jnp.concatenate([t, jnp.cos(ang), -jnp.sin(ang)], axis=-1)
    fr = freq.astype(f32)
    hdn = jnp.sin(fr * (feats @ w1.astype(f32) + b1.astype(f32)))
    hdn = jnp.sin(fr * (hdn @ w2.astype(f32) + b2.astype(f32)))
    hdn = jnp.sin(fr * (hdn @ w3.astype(f32) + b3.astype(f32)))
    filt = (hdn @ w4.astype(f32)).reshape(L, HYENA_ORDER, HYENA_DIRS, HYENA_WIDTH)
    deltas = jnp.abs(jnp.linspace(MIN_DECAY, MAX_DECAY, HYENA_WIDTH, dtype=f32))
    filt = filt * jnp.exp(-t * deltas)[:, None, None, :]
    fwd = filt[:, :, 0]
    bwd = filt[:, :, 1]
    zero = jnp.zeros((1, HYENA_ORDER, HYENA_WIDTH), f32)
    return jnp.concatenate([fwd, zero, bwd[1:][::-1]], axis=0)


def long_conv(u, kern2l, bias):
    L = u.shape[1]
    uf = jnp.fft.rfft(u.astype(jnp.float32), n=2 * L, axis=1)
    kf = jnp.fft.rfft(kern2l, n=2 * L, axis=0)
    y = jnp.fft.irfft(uf * kf[None], n=2 * L, axis=1)[:, :L]
    return (y + u.astype(jnp.float32) * bias.astype(jnp.float32)).astype(u.dtype)


def rope_partial(x, cos, sin):
    half = ROPE_DIM // 2
    xf = x.astype(jnp.float32)
    x1 = xf[..., :half]
    x2 = xf[..., half:ROPE_DIM]
    rot = jnp.concatenate([x1 * cos - x2 * sin, x2 * cos + x1 * sin], axis=-1)
    return jnp.concatenate([rot, xf[..., ROPE_DIM:]], axis=-1).astype(x.dtype)


def rope_tables(L):
    inv = ROPE_THETA ** (-jnp.arange(0, ROPE_DIM, 2, dtype=jnp.float32) / ROPE_DIM)
    ang = jnp.arange(L, dtype=jnp.float32)[:, None] * inv[None, :]
    return jnp.cos(ang)[:, None, :], jnp.sin(ang)[:, None, :]


def banded_sink_attention(q, k, v, sink):
    B, L = q.shape[0], q.shape[1]
    nb = L // BLOCK
    qb = q.reshape(B, nb, BLOCK, N_KV_HEADS, GROUP, HEAD_DIM)
    pad = ((0, 0), (BLOCK, BLOCK), (0, 0), (0, 0))
    kb = jnp.pad(k, pad).reshape(B, nb + 2, BLOCK, N_KV_HEADS, HEAD_DIM)
    vb = jnp.pad(v, pad).reshape(B, nb + 2, BLOCK, N_KV_HEADS, HEAD_DIM)
    kw = jnp.concatenate([kb[:, :-2], kb[:, 1:-1], kb[:, 2:]], axis=2)
    vw = jnp.concatenate([vb[:, :-2], vb[:, 1:-1], vb[:, 2:]], axis=2)
    s = jnp.einsum("bnqkgd,bnskd->bnkgqs", qb, kw, preferred_element_type=jnp.float32)
    s = s * (HEAD_DIM ** -0.5)
    blk = jnp.arange(nb)[:, None, None]
    qpos = blk * BLOCK + jnp.arange(BLOCK)[None, :, None]
    kpos = (blk - 1) * BLOCK + jnp.arange(3 * BLOCK)[None, None, :]
    valid = (jnp.abs(kpos - qpos) <= WINDOW) & (kpos >= 0) & (kpos < L)
    s = jnp.where(valid[None, :, None, None], s, -jnp.inf)
    sk = sink.astype(jnp.float32).reshape(1, 1, N_KV_HEADS, GROUP, 1, 1)
    m = jnp.maximum(jnp.max(s, axis=-1, keepdims=True), sk)
    p = jnp.exp(s - m)
    p = p / (jnp.sum(p, axis=-1, keepdims=True) + jnp.exp(sk - m))
    o = jnp.einsum("bnkgqs,bnskd->bnqkgd", p.astype(vw.dtype), vw)
    return o.reshape(B, L, N_HEADS * HEAD_DIM)


def hybrid_layer(x, norm_g, w_in, conv_w, conv_b, filt_w1, filt_b1, filt_w2, filt_b2,
                 filt_w3, filt_b3, filt_w4, filt_freq, hyena_bias, attn_sink,
                 w_hyena_out, w_attn_out, w_out):
    B, L, _ = x.shape
    h = rms_norm(x, norm_g)
    proj = h @ w_in
    points = np.cumsum(IN_SIZES)[:-1].tolist()
    u_hy, z_hy, q, k, v, z_at, g_hy, g_at = jnp.split(proj, points, axis=-1)

    u_hy = short_conv_centred(u_hy, conv_w, conv_b)
    hv, hx1, hx2 = jnp.split(u_hy, HYENA_ORDER + 1, axis=-1)
    kern = hyena_kernels(L, filt_w1, filt_b1, filt_w2, filt_b2, filt_w3, filt_b3, filt_w4, filt_freq)
    z = hx1 * long_conv(hv, kern[:, 0], hyena_bias[0])
    y_hy = hx2 * long_conv(z, kern[:, 1], hyena_bias[1])
    y_hy = y_hy * jax.nn.silu(z_hy)

    cos, sin = rope_tables(L)
    q = rope_partial(q.reshape(B, L, N_HEADS, HEAD_DIM), cos, sin)
    k = rope_partial(k.reshape(B, L, N_KV_HEADS, HEAD_DIM), cos, sin)
    v = v.reshape(B, L, N_KV_HEADS, HEAD_DIM)
    y_at = banded_sink_attention(q, k, v, attn_sink) * jax.nn.silu(z_at)

    merged = jax.nn.sigmoid(g_hy) * (y_hy @ w_hyena_out) + jax.nn.sigmoid(g_at) * (y_at @ w_attn_out)
    return x + merged @ w_out


def setup_inputs(seed: int = 0) -> dict:
    key = jax.random.key(seed)
    ks = jax.random.split(key, 20)
    f32 = jnp.float32

    def nrm(k, shape, scale):
        return jax.random.normal(k, shape, f32) * scale

    hw3 = (HYENA_ORDER + 1) * HYENA_WIDTH
    return {
        "x": nrm(ks[0], (BATCH, SEQ, D_MODEL), 1.0),
        "norm_g": 1.0 + nrm(ks[1], (DEPTH, D_MODEL), 0.02),
        "w_in": nrm(ks[2], (DEPTH, D_MODEL, IN_WIDTH), D_MODEL ** -0.5),
        "conv_w": nrm(ks[3], (DEPTH, SHORT_CONV, hw3), SHORT_CONV ** -0.5),
        "conv_b": nrm(ks[4], (DEPTH, hw3), 0.02),
        "filt_w1": nrm(ks[5], (DEPTH, FILTER_EMB, FILTER_HIDDEN), FILTER_EMB ** -0.5),
        "filt_b1": nrm(ks[6], (DEPTH, FILTER_HIDDEN), 0.1),
        "filt_w2": nrm(ks[7], (DEPTH, FILTER_HIDDEN, FILTER_HIDDEN), FILTER_HIDDEN ** -0.5),
        "filt_b2": nrm(ks[8], (DEPTH, FILTER_HIDDEN), 0.1),
        "filt_w3": nrm(ks[9], (DEPTH, FILTER_HIDDEN, FILTER_HIDDEN), FILTER_HIDDEN ** -0.5),
        "filt_b3": nrm(ks[10], (DEPTH, FILTER_HIDDEN), 0.1),
        "filt_w4": nrm(ks[11], (DEPTH, FILTER_HIDDEN, HYENA_ORDER * HYENA_DIRS * HYENA_WIDTH), 0.05 * FILTER_HIDDEN ** -0.5),
        "filt_freq": FILTER_SIN_W + nrm(ks[12], (DEPTH, FILTER_HIDDEN), 0.02),
        "hyena_bias": nrm(ks[13], (DEPTH, HYENA_ORDER, HYENA_WIDTH), 1.0),
        "attn_sink": nrm(ks[14], (DEPTH, N_HEADS), 0.5),
        "w_hyena_out": nrm(ks[15], (DEPTH, HYENA_WIDTH, D_MODEL), HYENA_WIDTH ** -0.5),
        "w_attn_out": nrm(ks[16], (DEPTH, ATTN_WIDTH, D_MODEL), ATTN_WIDTH ** -0.5),
        "w_out": nrm(ks[17], (DEPTH, D_MODEL, D_MODEL), D_MODEL ** -0.5),
        "final_norm": 1.0 + nrm(ks[18], (D_MODEL,), 0.02),
    }


def reference(x, norm_g, w_in, conv_w, conv_b, filt_w1, filt_b1, filt_w2, filt_b2,
              filt_w3, filt_b3, filt_w4, filt_freq, hyena_bias, attn_sink,
              w_hyena_out, w_attn_out, w_out, final_norm):
    for l in range(DEPTH):
        x = hybrid_layer(x, norm_g[l], w_in[l], conv_w[l], conv_b[l],
                         filt_w1[l], filt_b1[l], filt_w2[l], filt_b2[l],
                         filt_w3[l], filt_b3[l], filt_w4[l], filt_freq[l],
                         hyena_bias[l], attn_sink[l],
                         w_hyena_out[l], w_attn_out[l], w_out[l])
    return rms_norm(x, final_norm)
```

```python
import contextlib
import numpy as np
import ml_dtypes
import concourse.bass as bass
import concourse.mybir as mybir
from concourse.bass_utils import run_bass_kernel_spmd

F32 = mybir.dt.float32
BF16 = mybir.dt.bfloat16
AF = mybir.ActivationFunctionType
ALU = mybir.AluOpType
AX = mybir.AxisListType


class Prog:
    CE = ("pe", "act", "dve", "pool")

    def __init__(self, nc):
        self.nc = nc
        self.eng = {"pe": nc.tensor, "act": nc.scalar, "dve": nc.vector,
                    "pool": nc.gpsimd, "sp": nc.sync}
        self.csem = {e: nc.alloc_semaphore("s_" + e) for e in self.CE}
        self.cnt = {e: 0 for e in self.CE}
        self.pending = {e: False for e in self.CE}
        self.seen = {}
        self.kw = {}
        self.kr = {}
        self.dsem = {}
        self.dlast = {}
        self.stack = contextlib.ExitStack()
        self.phase_stack = None
        self.nops = 0

    def sb(self, name, shape, dtype, perm=False):
        st = self.stack if (perm or self.phase_stack is None) else self.phase_stack
        self.uid = getattr(self, "uid", 0) + 1
        return st.enter_context(self.nc.sbuf_tensor(f"sb{self.uid}_{name}", list(shape), dtype))

    def ps(self, name, shape, dtype=F32, perm=False):
        st = self.stack if (perm or self.phase_stack is None) else self.phase_stack
        self.uid = getattr(self, "uid", 0) + 1
        return st.enter_context(self.nc.psum_tensor(f"ps{self.uid}_{name}", list(shape), dtype))

    def phase_begin(self):
        self.scopes = getattr(self, "scopes", [])
        self.scopes.append(self.phase_stack)
        self.phase_stack = contextlib.ExitStack()

    def phase_end(self):
        self.barrier()
        self.phase_stack.close()
        self.phase_stack = self.scopes.pop()

    def _wait(self, F, tok):
        semname, sem, val, E = tok
        if E == "pe" and F == "pe":
            return
        k = (F, semname)
        if self.seen.get(k, 0) >= val:
            return
        self.eng[F].wait_ge(sem, val)
        self.seen[k] = val

    def _deps(self, F, r, w, nowaw):
        for k in r:
            for tok in self.kw.get(k, {}).values():
                self._wait(F, tok)
        for k in w:
            for tok in self.kr.get(k, {}).values():
                if tok[3] == F and F is not None and F in self.CE:
                    continue
                self._wait(F, tok)
            if not nowaw:
                for tok in self.kw.get(k, {}).values():
                    if tok[3] == F and F in self.CE:
                        continue
                    self._wait(F, tok)

    def _record(self, tok, r, w, nowaw):
        semname = tok[0]
        for k in w:
            if nowaw:
                self.kw.setdefault(k, {})[semname] = tok
            else:
                self.kw[k] = {semname: tok}
            self.kr[k] = {}
        for k in r:
            self.kr.setdefault(k, {})[semname] = tok

    def op(self, F, fn, r=(), w=(), nowaw=False, inc=True):
        self._deps(F, r, w, nowaw)
        ins = fn()
        sem = self.csem[F]
        if inc:
            self.cnt[F] += 1
            ins.then_inc(sem, 1)
            self.pending[F] = False
            val = self.cnt[F]
        else:
            self.pending[F] = True
            val = self.cnt[F] + 1
        self._record(("s_" + F, sem, val, F), r, w, nowaw)
        self.nops += 1
        return ins

    def dma(self, Q, out, in_, r=(), w=(), sem=None, nowaw=True, **kw):
        semname = "d_" + (sem if sem is not None else (w[0] if w else r[0]))
        if semname not in self.dsem:
            self.dsem[semname] = [self.nc.alloc_semaphore(semname), 0]
        ent = self.dsem[semname]
        if semname in self.dlast:
            s, v = self.dlast[semname]
            self._wait(Q, (semname, s, v, None))
        self._deps(Q, r, w, nowaw)
        ins = self.eng[Q].dma_start(out=out, in_=in_, **kw)
        ent[1] += 16
        ins.then_inc(ent[0], 16)
        tok = (semname, ent[0], ent[1], None)
        self.dlast[semname] = (ent[0], ent[1])
        self._record(tok, r, w, nowaw)
        self.nops += 1
        return ins

    def barrier(self, engines=("pe", "act", "dve", "pool", "sp")):
        for e in self.CE:
            assert not self.pending[e], e
        for F in engines:
            for E in self.CE:
                if self.cnt[E] > 0:
                    self._wait(F, ("s_" + E, self.csem[E], self.cnt[E], None))
            for semname, (s, v) in self.dlast.items():
                self._wait(F, (semname, s, v, None))

    def finish(self):
        self.barrier()
        while self.phase_stack is not None:
            self.phase_stack.close()
            self.phase_stack = self.scopes.pop() if getattr(self, "scopes", None) else None
        self.stack.close()


L = 2048
D = 2048
C = 1024
NW = 10752
MAGIC = 12582912.0
TWO_PI = 2.0 * np.pi

_CONSTS = None


def make_consts():
    global _CONSTS
    if _CONSTS is not None:
        return _CONSTS
    bf = ml_dtypes.bfloat16
    s = np.arange(L, dtype=np.float64)
    w = np.pi * (2 * np.arange(L, dtype=np.float64) + 1) / (2 * L)
    ang = np.outer(s, w)
    Cm = np.cos(ang)
    Sm = np.sin(ang)
    M = np.empty((L, 32, 128))
    M[:, 0::2] = Cm.reshape(L, 16, 128)
    M[:, 1::2] = -Sm.reshape(L, 16, 128)
    FT = np.ascontiguousarray(M.reshape(16, 128, 32, 128).transpose(2, 1, 0, 3)).astype(bf)
    G = np.empty((32, 128, L))
    G[0::2] = Cm.T.reshape(16, 128, L) * (2.0 / (2 * L))
    G[1::2] = -Sm.T.reshape(16, 128, L) * (2.0 / (2 * L))
    GT = np.ascontiguousarray(G.reshape(32, 128, 16, 128).transpose(2, 1, 0, 3)).astype(bf)
    del M, G, Cm, Sm, ang
    f32 = np.float32
    t = np.linspace(0.0, 1.0, L, dtype=f32)[:, None]
    bands = np.linspace(1e-4, 15.0, 16, dtype=f32)[None, :]
    a = (f32(2.0 * np.pi / L) * np.arange(L, dtype=f32)[:, None]) * bands
    feats = np.concatenate([t, np.cos(a), -np.sin(a)], axis=-1).astype(f32)
    featsT = np.ascontiguousarray(feats.T)
    mind = np.log(1e-2) / 0.3
    maxd = np.log(1e-2) / 1.5
    deltas = np.abs(np.linspace(mind, maxd, C, dtype=f32))
    dec = np.exp(-t * deltas[None, :]).astype(f32)
    dec_b = dec.copy()
    dec_b[0] = 0.0
    decay = np.stack([dec, dec_b], axis=1)
    decay = np.ascontiguousarray(decay)
    inv = (500000.0 ** (-np.arange(0, 32, 2, dtype=f32) / f32(32))).astype(f32)
    ra = np.arange(L, dtype=f32)[None, :] * inv[:, None]
    ropeC = np.ones((128, L), f32)
    ropeS = np.zeros((128, L), f32)
    ropeC[0:16] = np.cos(ra)
    ropeC[16:32] = np.cos(ra)
    ropeS[0:16] = np.sin(ra)
    ropeS[16:32] = np.sin(ra)
    RT = np.zeros((128, 128), f32)
    for i in range(16):
        RT[i + 16, i] = -1.0
        RT[i, i + 16] = 1.0
    kl = np.arange(128)[:, None]
    ql = np.arange(128)[None, :]
    m0 = (kl >= ql).astype(f32)
    m1 = (kl <= ql).astype(f32)
    masks = np.stack([np.tile(m0, (1, 4)), np.tile(m1, (1, 4))], axis=1)
    _CONSTS = dict(FT=FT, GT=GT, featsT=featsT, decay=decay, ropeC=ropeC, ropeS=ropeS,
                   RT=RT.astype(bf), masks=np.ascontiguousarray(masks).astype(bf))
    return _CONSTS


def build(nlayers=2, dbg=()):
    nc = bass.Bass("TRN2", target_bir_lowering=False)
    P = Prog(nc)
    act, dve, pool, pe = nc.scalar, nc.vector, nc.gpsimd, nc.tensor

    def din(name, shape, dt=F32):
        return nc.dram_tensor(name, list(shape), dt, kind="ExternalInput").ap()

    def dscr(name, shape, dt):
        kind = "ExternalOutput" if name in dbg else "Internal"
        return nc.dram_tensor(name, list(shape), dt, kind=kind).ap()

    x_in = din("x", [L, D])
    norm_g = din("norm_g", [2, D])
    final_norm = din("final_norm", [1, D])
    w_in = din("w_in", [2, D, NW])
    conv_wp = din("conv_wp", [2, 128, 3, 24])
    conv_b = din("conv_b", [2, 3 * C])
    conv_bp = din("conv_bp", [2, 128, 24])
    fw1 = din("filt_w1", [2, 33, 64])
    fw2 = din("filt_w2", [2, 64, 64])
    fw3 = din("filt_w3", [2, 64, 64])
    fw4 = din("filt_w4", [2, 64, 4 * C])
    fsc = din("filt_sc", [2, 64, 4])
    hbias = din("hyena_bias", [2, 2 * C])
    sinkx = din("sinkx", [2, 2, 512])
    w_hy = din("w_hyena_out", [2, C, D])
    w_at = din("w_attn_out", [2, C, D])
    w_out = din("w_out", [2, D, D])
    FT = din("FT", [32, 128, 16, 128], BF16)
    GT = din("GT", [16, 128, 32, 128], BF16)
    featsT = din("featsT", [33, L])
    decay = din("decay", [L, 2, C])
    ropeC = din("ropeC", [128, L])
    ropeS = din("ropeS", [128, L])
    RTd = din("RT", [128, 128], BF16)
    masksd = din("masks", [128, 2, 512], BF16)

    out = nc.dram_tensor("out", [L, D], F32, kind="ExternalOutput").ap()
    Kf = dscr("Kf", [2, 32, 128, C], F32)
    x2T = dscr("x2T", [8, 128, L], BF16)
    zhyT = dscr("zhyT", [8, 128, L], BF16)
    qT = dscr("qT", [8, 128, L], BF16)
    kT = dscr("kT", [2, 128, L], BF16)
    zatT = dscr("zatT", [8, 128, L], BF16)
    sgT = dscr("sgT", [32, 128, L], BF16)
    yhyT = dscr("yhyT", [8, 128, L], BF16)
    yatT = dscr("yatT", [8, 128, L], BF16)
    mrgT = dscr("mrgT", [16, 128, L], BF16)
    xa = dscr("xa", [L, D], F32)
    dbg_vx = dscr("dbg_vx", [128, 2, 16, C], BF16) if "dbg_vx" in dbg else None
    dbg_vat = dscr("dbg_vat", [128, 16, 256], BF16) if "dbg_vat" in dbg else None

    ident = P.sb("ident", [128, 128], BF16, perm=True)
    ones = P.sb("ones", [128, 128], BF16, perm=True)
    hold = {}
    P.op("pool", lambda: pool.memset(ident[:], 1.0), w=["ident"])
    P.op("pool", lambda: pool.affine_select(ident[:], ident[:], pattern=[[-1, 128]],
                                            compare_op=ALU.is_equal, fill=0.0, base=0,
                                            channel_multiplier=1), r=["ident"], w=["ident"])
    P.op("pool", lambda: pool.memset(ones[:], 1.0), w=["ones"])

    rr = {"n": 0}

    def rot(n):
        rr["n"] += 1
        return rr["n"] % n

    def sin_layer(lhsT_ap, rhs_fn, freq_ap, fb_ap, ag, md, hout, pss):
        for q in range(4):
            ps = pss[q % 2]
            psk = f"fps{q % 2}"
            P.op("pe", lambda: pe.matmul(ps[:], lhsT_ap, rhs_fn(q), start=True, stop=True),
                 r=["fw", "fh"], w=[psk])
            P.op("dve", lambda: dve.tensor_scalar(ag[:, q * 512:(q + 1) * 512], ps[:], freq_ap, fb_ap,
                                                  op0=ALU.mult, op1=ALU.add), r=[psk, "fsc"], w=["ag"], nowaw=True)
        P.op("dve", lambda: dve.tensor_scalar(md[:], ag[:], 1.0 / TWO_PI, MAGIC, op0=ALU.mult, op1=ALU.add),
             r=["ag"], w=["md"])
        P.op("dve", lambda: dve.tensor_scalar(md[:], md[:], MAGIC, None, op0=ALU.subtract), r=["md"], w=["md"])
        P.op("dve", lambda: dve.scalar_tensor_tensor(md[:], md[:], -TWO_PI, ag[:], op0=ALU.mult, op1=ALU.add),
             r=["md", "ag"], w=["md"])
        P.op("act", lambda: act.activation(hout[:], md[:], AF.Sin, scale=0.999999), r=["md"], w=["fh"])

    def phase_filter(l):
        P.phase_begin()
        fT = P.sb("featsT", [33, L], F32)
        w1 = P.sb("fw1", [33, 64], F32)
        w2 = P.sb("fw2", [64, 64], F32)
        w3 = P.sb("fw3", [64, 64], F32)
        w4 = P.sb("fw4", [64, 4 * C], F32)
        sc = P.sb("fsc", [64, 4], F32)
        fb = P.sb("ffb", [64, 3], F32)
        ag = P.sb("fag", [64, L], F32)
        md = P.sb("fmd", [64, L], F32)
        hA = P.sb("fhA", [64, L], F32)
        hB = P.sb("fhB", [64, L], F32)
        hb_bc = P.sb("hb_bc", [128, 2 * C], F32)
        ksum = P.sb("ksum", [128, 16, C], BF16)
        kdif = P.sb("kdif", [128, 16, C], BF16)
        pss = [P.ps(f"fps{i}", [128, 512], F32) for i in range(4)]
        P.dma("sp", fT[:], featsT, w=["fh"], sem="f_fT")
        P.dma("sp", w1[:], fw1[l], w=["fw"], sem="f_w1")
        P.dma("sp", w2[:], fw2[l], w=["fw"], sem="f_w2")
        P.dma("sp", w3[:], fw3[l], w=["fw"], sem="f_w3")
        P.dma("sp", w4[:], fw4[l], w=["fw"], sem="f_w4")
        P.dma("sp", sc[:], fsc[l], w=["fsc"], sem="f_sc")
        P.dma("sp", hb_bc[:], hbias[l:l + 1, :].partition_broadcast(128), w=["hb_bc"])
        P.op("dve", lambda: dve.tensor_scalar(fb[:], sc[:, 1:4], sc[:, 0:1], None, op0=ALU.mult),
             r=["fsc"], w=["fsc2"])
        P.kw["fsc"].update(P.kw["fsc2"])
        ps64 = [p_[0:64, :] for p_ in pss]
        sin_layer(w1[:], lambda q: fT[:, q * 512:(q + 1) * 512], sc[:, 0:1], fb[:, 0:1], ag, md, hA, ps64)
        sin_layer(w2[:], lambda q: hA[:, q * 512:(q + 1) * 512], sc[:, 0:1], fb[:, 1:2], ag, md, hB, ps64)
        sin_layer(w3[:], lambda q: hB[:, q * 512:(q + 1) * 512], sc[:, 0:1], fb[:, 2:3], ag, md, hA, ps64)
        dct = [P.sb(f"dct{i}", [128, 2, C], F32) for i in range(2)]
        tf = [P.sb(f"ftf{i}", [128, 512], F32) for i in range(2)]
        tb_ = [P.sb(f"ftb{i}", [128, 512], F32) for i in range(2)]
        fts = [P.sb(f"fts{i}", [128, 16, 128], BF16) for i in range(3)]
        kst = [P.sb(f"kst{i}", [128, C], F32) for i in range(2)]
        nd = 0
        nf = 0
        for o in range(2):
            for sb in range(16):
                di = nd % 2
                nd += 1
                P.dma("sp", dct[di][:], decay[sb * 128:(sb + 1) * 128], w=[f"dct{di}"])
                for half in range(2):
                    i = rot(2)
                    c0 = half * 512
                    for dr in range(2):
                        n0 = o * 2048 + dr * 1024 + c0
                        ps = pss[2 * i + dr]
                        P.op("pe", lambda: pe.matmul(ps[:], hA[:, sb * 128:(sb + 1) * 128], w4[:, n0:n0 + 512],
                                                     start=True, stop=True), r=["fh", "fw"], w=[f"fps{2 * i + dr}"])
                        tt = (tf, tb_)[dr][i]
                        P.op("dve", lambda: dve.tensor_tensor(tt[:], ps[:], dct[di][:, dr, c0:c0 + 512], op=ALU.mult),
                             r=[f"fps{2 * i + dr}", f"dct{di}"], w=[f"ft{dr}{i}"])
                    P.op("pool", lambda: pool.tensor_tensor(ksum[:, sb, c0:c0 + 512], tf[i][:], tb_[i][:],
                                                            op=ALU.add), r=[f"ft0{i}", f"ft1{i}"], w=["ksum"], nowaw=True)
                    P.op("pool", lambda: pool.tensor_tensor(kdif[:, sb, c0:c0 + 512], tf[i][:], tb_[i][:],
                                                            op=ALU.subtract), r=[f"ft0{i}", f"ft1{i}"], w=["kdif"], nowaw=True)
            for frb in range(32):
                fi = nf % 3
                nf += 1
                P.dma("sp", fts[fi][:], FT[frb], w=[f"fts{fi}"])
                src = ksum if frb % 2 == 0 else kdif
                srck = "ksum" if frb % 2 == 0 else "kdif"
                ki = frb % 2
                for nq in range(2):
                    pi = rot(4)
                    ps = pss[pi]
                    for sc_ in range(16):
                        P.op("pe", lambda: pe.matmul(ps[:], fts[fi][:, sc_, :], src[:, sc_, nq * 512:(nq + 1) * 512],
                                                     start=(sc_ == 0), stop=(sc_ == 15)),
                             r=[f"fts{fi}", srck], w=[f"fps{pi}"], inc=(sc_ == 15))
                    if frb % 2 == 0:
                        P.op("dve", lambda: dve.tensor_tensor(kst[ki][:, nq * 512:(nq + 1) * 512], ps[:],
                                                              hb_bc[:, o * C + nq * 512:o * C + (nq + 1) * 512], op=ALU.add),
                             r=[f"fps{pi}", "hb_bc"], w=[f"kst{ki}"], nowaw=True)
                    else:
                        P.op("act", lambda: act.activation(kst[ki][:, nq * 512:(nq + 1) * 512], ps[:], AF.Copy),
                             r=[f"fps{pi}"], w=[f"kst{ki}"], nowaw=True)
                P.dma("sp", Kf[o, frb], kst[ki][:], r=[f"kst{ki}"], w=["Kf"], sem=f"kst{ki}")
        P.phase_end()

    def phase_proj(l, xsrc):
        vx_tok, v_at = hold["vx_tok"], hold["v_at"]
        P.phase_begin()
        hT = P.sb("hT", [128, 16, L], BF16)
        P.phase_begin()
        g_bc = P.sb("g_bc", [128, D], F32)
        P.dma("sp", g_bc[:], norm_g[l:l + 1, :].partition_broadcast(128), w=["g_bc"])
        xt = [P.sb(f"xt{i}", [128, D], F32) for i in range(2)]
        sq = P.sb("sq", [128, D], F32)
        ss = [P.sb(f"ss{i}", [128, 1], F32) for i in range(2)]
        xn = [P.sb(f"xn{i}", [128, D], BF16) for i in range(2)]
        pt = [P.ps(f"pt{i}", [128, 4, 128], BF16) for i in range(2)]
        for tb in range(16):
            i = tb % 2
            P.dma("sp", xt[i][:], xsrc[tb * 128:(tb + 1) * 128, :], w=[f"xt{i}"])
            P.op("act", lambda: act.activation(sq[:], xt[i][:], AF.Square, accum_out=ss[i][:]),
                 r=[f"xt{i}"], w=["sq", f"ss{i}"])
            P.op("dve", lambda: dve.tensor_scalar(ss[i][:], ss[i][:], 1.0 / D, 1e-6, op0=ALU.mult, op1=ALU.add),
                 r=[f"ss{i}"], w=[f"ss{i}"])
            P.op("act", lambda: act.sqrt(ss[i][:], ss[i][:]), r=[f"ss{i}"], w=[f"ss{i}"])
            P.op("dve", lambda: dve.reciprocal(ss[i][:], ss[i][:]), r=[f"ss{i}"], w=[f"ss{i}"])
            P.op("dve", lambda: dve.scalar_tensor_tensor(xn[i][:], xt[i][:], ss[i][:, 0:1], g_bc[:],
                                                         op0=ALU.mult, op1=ALU.mult),
                 r=[f"xt{i}", f"ss{i}", "g_bc"], w=[f"xn{i}"])
            for q in range(4):
                j = rot(2)
                for c in range(4):
                    dc = q * 4 + c
                    P.op("pe", lambda: pe.transpose(pt[j][:, c, :], xn[i][:, dc * 128:(dc + 1) * 128], ident[:]),
                         r=[f"xn{i}", "ident"], w=[f"pt{j}"], inc=(c == 3))
                P.op("act", lambda: act.activation(hT[:, q * 4:(q + 1) * 4, tb * 128:(tb + 1) * 128], pt[j][:], AF.Copy),
                     r=[f"pt{j}"], w=["hT"], nowaw=True)

        P.phase_end()
        wsl = [P.sb(f"wsl{i}", [128, 16, 256], BF16) for i in range(3)]
        upad = [P.sb(f"upad{i}", [128, L + 2], BF16) for i in range(2)]
        stg = [P.sb(f"stg{i}", [128, L], BF16) for i in range(3)]
        dg = [[P.sb(f"dg{i}_{j}", [128, 128], BF16) for j in range(3)] for i in range(2)]
        cw = P.sb("cw", [128, 3, 24], F32)
        cbp = P.sb("cbp", [128, 24], F32)
        cbrow = P.sb("cbrow", [1, 3 * C], BF16)
        P.dma("sp", cw[:], conv_wp[l], w=["cw"])
        P.dma("sp", cbp[:], conv_bp[l], w=["cbp"])
        P.dma("pool", cbrow[:], conv_b[l:l + 1, :], w=["cbrow"])
        for i in range(2):
            P.op("pool", lambda: pool.memset(upad[i][:, 0:1], 0.0), w=[f"upad{i}"], nowaw=True)
            P.op("pool", lambda: pool.memset(upad[i][:, L + 1:L + 2], 0.0), w=[f"upad{i}"], nowaw=True)
        pm = [P.ps(f"pm{i}", [128, 512], F32) for i in range(4)]
        pc = [P.ps(f"pc{i}", [128, 4, 128], F32) for i in range(2)]
        w_l = w_in[l].rearrange("(c p) n -> p c n", p=128)

        def load_slab(si):
            b = si % 3
            P.dma("pool", wsl[b][:], w_l[:, :, si * 256:(si + 1) * 256], w=[f"wsl{b}"])

        def fm_matmul(b, j, tq):
            pi = rot(4)
            for dc in range(16):
                P.op("pe", lambda: pe.matmul(pm[pi][:], wsl[b][:, dc, j * 128:(j + 1) * 128],
                                             hT[:, dc, tq * 512:(tq + 1) * 512],
                                             start=(dc == 0), stop=(dc == 15)),
                     r=[f"wsl{b}", "hT"], w=[f"pm{pi}"], inc=(dc == 15))
            return pi

        load_slab(0)
        load_slab(1)
        stg_n = 0
        for si in range(42):
            if si + 2 < 42:
                load_slab(si + 2)
            b = si % 3
            for j in range(2):
                n = 2 * si + j
                if n < 24:
                    ui = n % 2
                    for tq in range(4):
                        pi = fm_matmul(b, j, tq)
                        P.op("act", lambda: act.activation(upad[ui][:, 1 + tq * 512:1 + (tq + 1) * 512], pm[pi][:], AF.Copy),
                             r=[f"pm{pi}"], w=[f"upad{ui}"], nowaw=True)
                    for tap in range(3):
                        P.op("dve", lambda: dve.tensor_scalar(dg[ui][tap][:], ident[:], cw[:, tap, n:n + 1], None,
                                                              op0=ALU.mult), r=["ident", "cw"], w=[f"dg{ui}"], nowaw=True)
                    stream, cc = n // 8, n % 8
                    if stream < 2:
                        for t4 in range(4):
                            ci = rot(2)
                            for tt in range(4):
                                tb = t4 * 4 + tt
                                for tap in range(3):
                                    P.op("pe", lambda: pe.matmul(pc[ci][:, tt, :], upad[ui][:, tb * 128 + tap:tb * 128 + tap + 128],
                                                                 dg[ui][tap][:], start=(tap == 0), stop=False),
                                         r=[f"upad{ui}", f"dg{ui}"], w=[f"pc{ci}"], inc=False)
                                P.op("pe", lambda: pe.matmul(pc[ci][:, tt, :], ones[0:1, :], cbrow[0:1, n * 128:(n + 1) * 128],
                                                             start=False, stop=True),
                                     r=["ones", "cbrow"], w=[f"pc{ci}"], inc=(tt == 3))
                            P.op("act", lambda: act.activation(vx_tok[:, stream, t4 * 4:(t4 + 1) * 4, cc * 128:(cc + 1) * 128],
                                                               pc[ci][:], AF.Copy),
                                 r=[f"pc{ci}"], w=["vx_tok"], nowaw=True)
                    else:
                        sgi = stg_n % 3
                        stg_n += 1
                        for tq in range(4):
                            pi = rot(4)
                            for tap in range(3):
                                P.op("pe", lambda: pe.matmul(pm[pi][:], dg[ui][tap][:],
                                                             upad[ui][:, tq * 512 + tap:tq * 512 + tap + 512],
                                                             start=(tap == 0), stop=(tap == 2)),
                                     r=[f"upad{ui}", f"dg{ui}"], w=[f"pm{pi}"], inc=(tap == 2))
                            P.op("act", lambda: act.activation(stg[sgi][:, tq * 512:(tq + 1) * 512], pm[pi][:], AF.Identity,
                                                               bias=cbp[:, n:n + 1]),
                                 r=[f"pm{pi}", "cbp"], w=[f"stg{sgi}"], nowaw=True)
                        P.dma("sp", x2T[cc], stg[sgi][:], r=[f"stg{sgi}"], w=["x2T"], sem=f"stg{sgi}")
                elif n in (42, 43):
                    for t4 in range(4):
                        ci = rot(2)
                        for tt in range(4):
                            tb = t4 * 4 + tt
                            for dc in range(16):
                                P.op("pe", lambda: pe.matmul(pc[ci][:, tt, :], hT[:, dc, tb * 128:(tb + 1) * 128],
                                                             wsl[b][:, dc, j * 128:(j + 1) * 128],
                                                             start=(dc == 0), stop=(dc == 15)),
                                     r=[f"wsl{b}", "hT"], w=[f"pc{ci}"], inc=(dc == 15 and tt == 3))
                        P.op("act", lambda: act.activation(v_at[:, t4 * 4:(t4 + 1) * 4, (n - 42) * 128:(n - 41) * 128],
                                                           pc[ci][:], AF.Copy),
                             r=[f"pc{ci}"], w=["v_at"], nowaw=True)
                else:
                    if n < 32:
                        fn, dst, dk = AF.Silu, zhyT[n - 24], "zhyT"
                    elif n < 40:
                        fn, dst, dk = AF.Copy, qT[n - 32], "qT"
                    elif n < 42:
                        fn, dst, dk = AF.Copy, kT[n - 40], "kT"
                    elif n < 52:
                        fn, dst, dk = AF.Silu, zatT[n - 44], "zatT"
                    else:
                        fn, dst, dk = AF.Sigmoid, sgT[n - 52], "sgT"
                    sgi = stg_n % 3
                    stg_n += 1
                    for tq in range(4):
                        pi = fm_matmul(b, j, tq)
                        P.op("act", lambda: act.activation(stg[sgi][:, tq * 512:(tq + 1) * 512], pm[pi][:], fn),
                             r=[f"pm{pi}"], w=[f"stg{sgi}"], nowaw=True)
                    P.dma("sp", dst, stg[sgi][:], r=[f"stg{sgi}"], w=[dk], sem=f"stg{sgi}")
        if dbg_vx is not None:
            P.dma("sp", dbg_vx, vx_tok[:], r=["vx_tok"], w=["dbg_vx"], sem="dbgvx")
        if dbg_vat is not None:
            P.dma("sp", dbg_vat, v_at[:], r=["v_at"], w=["dbg_vat"], sem="dbgvat")
        P.phase_end()

    def phase_hyena(l):
        vx_tok = hold["vx_tok"]
        P.phase_begin()
        Y = P.sb("Y", [128, 32, 512], BF16)
        z_tok = P.sb("z_tok", [128, 16, 512], BF16)
        fts = [P.sb(f"hfts{i}", [128, 16, 128], BF16) for i in range(3)]
        gts = [P.sb(f"gts{i}", [128, 32, 128], BF16) for i in range(2)]
        gt2 = [P.sb(f"gt2_{i}", [128, 32, 128], BF16) for i in range(2)]
        kri = [P.sb(f"kri{i}", [128, 2, 512], F32) for i in range(2)]
        tmp = [[P.sb(f"htmp{i}_{j}", [128, 512], F32) for j in range(4)] for i in range(2)]
        x2t = [P.sb(f"x2t{i}", [128, 4, 128], BF16) for i in range(2)]
        zht = [P.sb(f"zht{i}", [128, 4, 128], BF16) for i in range(2)]
        yt = [P.sb(f"yt{i}", [128, 4, 128], BF16) for i in range(2)]
        ytmp = [P.sb(f"ytmp{i}", [128, 128], F32) for i in range(2)]
        pu = [P.ps(f"pu{i}", [128, 512], F32) for i in range(4)]
        pz = [P.ps(f"pz{i}", [128, 512], F32) for i in range(2)]
        py = [P.ps(f"py{i}", [128, 128], F32) for i in range(2)]
        nft = {"n": 0}

        def dft(src_fn, srck, o, c0):
            for fb in range(16):
                ki = fb % 2
                P.dma("sp", kri[ki][:], Kf[o, 2 * fb:2 * fb + 2, :, c0:c0 + 512].rearrange("k p c -> p k c"),
                      r=["Kf"], w=[f"kri{ki}"])
                pis = []
                for kind in range(2):
                    frb = 2 * fb + kind
                    fi = nft["n"] % 3
                    nft["n"] += 1
                    P.dma("sp", fts[fi][:], FT[frb], w=[f"hfts{fi}"])
                    pi = 2 * (fb % 2) + kind
                    pis.append(pi)
                    for sc_ in range(16):
                        P.op("pe", lambda: pe.matmul(pu[pi][:], fts[fi][:, sc_, :], src_fn(sc_),
                                                     start=(sc_ == 0), stop=(sc_ == 15)),
                             r=[f"hfts{fi}", srck], w=[f"pu{pi}"], inc=(sc_ == 15))
                ur, ui = pu[pis[0]], pu[pis[1]]
                urk, uik = f"pu{pis[0]}", f"pu{pis[1]}"
                t = tmp[ki]
                tk = f"htmp{ki}"
                kr_, ki_ = kri[ki][:, 0, :], kri[ki][:, 1, :]
                P.op("dve", lambda: dve.tensor_tensor(t[0][:], ur[:], kr_, op=ALU.mult), r=[urk, f"kri{ki}"], w=[tk + "a"])
                P.op("dve", lambda: dve.tensor_tensor(t[1][:], ui[:], ki_, op=ALU.mult), r=[uik, f"kri{ki}"], w=[tk + "b"])
                P.op("dve", lambda: dve.tensor_tensor(t[2][:], ur[:], ki_, op=ALU.mult), r=[urk, f"kri{ki}"], w=[tk + "c"])
                P.op("dve", lambda: dve.tensor_tensor(t[3][:], ui[:], kr_, op=ALU.mult), r=[uik, f"kri{ki}"], w=[tk + "d"])
                P.op("pool", lambda: pool.tensor_tensor(Y[:, 2 * fb, :], t[0][:], t[1][:], op=ALU.subtract),
                     r=[tk + "a", tk + "b"], w=["Y"], nowaw=True)
                P.op("pool", lambda: pool.tensor_tensor(Y[:, 2 * fb + 1, :], t[2][:], t[3][:], op=ALU.add),
                     r=[tk + "c", tk + "d"], w=["Y"], nowaw=True)

        for g in range(2):
            c0 = g * 512
            dft(lambda sc_: vx_tok[:, 0, sc_, c0:c0 + 512], "vx_tok", 0, c0)
            for tb in range(16):
                gi = tb % 2
                P.dma("sp", gts[gi][:], GT[tb], w=[f"gts{gi}"])
                for frc in range(32):
                    P.op("pe", lambda: pe.matmul(pz[gi][:], gts[gi][:, frc, :], Y[:, frc, :],
                                                 start=(frc == 0), stop=(frc == 31)),
                         r=[f"gts{gi}", "Y"], w=[f"pz{gi}"], inc=(frc == 31))
                P.op("dve", lambda: dve.tensor_tensor(z_tok[:, tb, :], pz[gi][:], vx_tok[:, 1, tb, c0:c0 + 512], op=ALU.mult),
                     r=[f"pz{gi}", "vx_tok"], w=["z_tok"], nowaw=True)
            dft(lambda sc_: z_tok[:, sc_, :], "z_tok", 1, c0)
            for tq in range(16):
                gi = tq % 2
                P.dma("sp", gt2[gi][:], GT[tq], w=[f"gt2_{gi}"])
                P.dma("sp", x2t[gi][:], x2T[g * 4:(g + 1) * 4, :, tq * 128:(tq + 1) * 128].rearrange("c p t -> p c t"),
                      r=["x2T"], w=[f"x2t{gi}"])
                P.dma("sp", zht[gi][:], zhyT[g * 4:(g + 1) * 4, :, tq * 128:(tq + 1) * 128].rearrange("c p t -> p c t"),
                      r=["zhyT"], w=[f"zht{gi}"])
                for cc in range(4):
                    pi = rot(2)
                    for frc in range(32):
                        P.op("pe", lambda: pe.matmul(py[pi][:], Y[:, frc, cc * 128:(cc + 1) * 128], gt2[gi][:, frc, :],
                                                     start=(frc == 0), stop=(frc == 31)),
                             r=[f"gt2_{gi}", "Y"], w=[f"py{pi}"], inc=(frc == 31))
                    P.op("dve", lambda: dve.tensor_tensor(ytmp[pi][:], py[pi][:], x2t[gi][:, cc, :], op=ALU.mult),
                         r=[f"py{pi}", f"x2t{gi}"], w=[f"ytmp{pi}"])
                    P.op("pool", lambda: pool.tensor_tensor(yt[gi][:, cc, :], ytmp[pi][:], zht[gi][:, cc, :], op=ALU.mult),
                         r=[f"ytmp{pi}", f"zht{gi}"], w=[f"yt{gi}"], nowaw=True)
                P.dma("sp", yhyT[g * 4:(g + 1) * 4, :, tq * 128:(tq + 1) * 128].rearrange("c p t -> p c t"), yt[gi][:],
                      r=[f"yt{gi}"], w=["yhyT"], sem=f"yt{gi}")
        P.phase_end()

    def phase_attn(l):
        v_at = hold["v_at"]
        P.phase_begin()
        cosT = P.sb("cosT", [128, L], F32)
        sinT = P.sb("sinT", [128, L], F32)
        RT = P.sb("RT", [128, 128], BF16)
        masks = P.sb("masks", [128, 2, 512], BF16)
        P.dma("sp", cosT[:], ropeC, w=["cosT"])
        P.dma("sp", sinT[:], ropeS, w=["sinT"])
        P.dma("sp", RT[:], RTd, w=["RT"])
        P.dma("sp", masks[:], masksd, w=["masks"])
        raw = [P.sb(f"raw{i}", [128, L], BF16) for i in range(2)]
        kr = P.sb("kr", [128, L], BF16)
        qr = P.sb("qr", [128, 4, L], BF16)
        zat = P.sb("zat", [128, 4, L], BF16)
        yat = P.sb("yat", [128, 4, L], BF16)
        esink = P.sb("esink", [128, 512], F32)
        rt1 = [P.sb(f"rt1_{i}", [128, 512], F32) for i in range(2)]
        rt2 = [P.sb(f"rt2_{i}", [128, 512], F32) for i in range(2)]
        PT = [P.sb(f"PT{i}", [128, 512], BF16) for i in range(6)]
        den = [P.sb(f"den{i}", [128, 512], F32) for i in range(2)]
        otmp = [P.sb(f"otmp{i}", [128, 512], F32) for i in range(2)]
        p_r = P.ps("p_r", [128, 512], F32)
        p_s = [P.ps(f"p_s{i}", [128, 512], F32) for i in range(3)]
        p_o = [P.ps(f"p_o{i}", [128, 512], F32) for i in range(2)]
        p_d = [P.ps(f"p_d{i}", [128, 512], F32) for i in range(2)]
        nraw = {"n": 0}
        scale = 1.0 / np.sqrt(128.0)

        def rope(src_dram, dst_fn, dstk):
            ri = nraw["n"] % 2
            nraw["n"] += 1
            P.dma("sp", raw[ri][:], src_dram, w=[f"raw{ri}"])
            for tq in range(4):
                sl = slice(tq * 512, (tq + 1) * 512)
                ti = rot(2)
                P.op("pe", lambda: pe.matmul(p_r[:], RT[:], raw[ri][:, sl], start=True, stop=True),
                     r=["RT", f"raw{ri}"], w=["p_r"])
                P.op("pool", lambda: pool.tensor_tensor(rt1[ti][:], raw[ri][:, sl], cosT[:, sl], op=ALU.mult),
                     r=[f"raw{ri}", "cosT"], w=[f"rt1_{ti}"])
                P.op("dve", lambda: dve.tensor_tensor(rt2[ti][:], p_r[:], sinT[:, sl], op=ALU.mult),
                     r=["p_r", "sinT"], w=[f"rt2_{ti}"])
                P.op("pool", lambda: pool.tensor_tensor(dst_fn(sl), rt1[ti][:], rt2[ti][:], op=ALU.add),
                     r=[f"rt1_{ti}", f"rt2_{ti}"], w=[dstk], nowaw=True)

        npt = {"n": 0}
        for g in range(2):
            rope(kT[g], lambda sl: kr[:, sl], "kr")
            for h in range(4):
                rope(qT[g * 4 + h], lambda sl: qr[:, h, sl], "qr")
            P.dma("sp", esink[:], sinkx[l, g:g + 1, :].partition_broadcast(128), w=["esink"])
            P.op("act", lambda: act.activation(esink[:], esink[:], AF.Exp), r=["esink"], w=["esink"])
            P.dma("sp", zat[:], zatT[g * 4:(g + 1) * 4].rearrange("c p t -> p c t"), r=["zatT"], w=["zat"])
            for qb in range(16):
                kbs = [kb for kb in (qb - 1, qb, qb + 1) if 0 <= kb < 16]
                qsl = slice(qb * 128, (qb + 1) * 128)
                pts = []
                for i, kb in enumerate(kbs):
                    pti = npt["n"] % 6
                    npt["n"] += 1
                    pts.append(pti)
                    P.op("pe", lambda: pe.matmul(p_s[i][:].rearrange("p (h q) -> p h q", h=4), kr[:, kb * 128:(kb + 1) * 128],
                                                 qr[:, :, qsl], start=True, stop=True),
                         r=["kr", "qr"], w=[f"p_s{i}"])
                    P.op("act", lambda: act.activation(PT[pti][:], p_s[i][:], AF.Exp, scale=float(scale)),
                         r=[f"p_s{i}"], w=[f"PT{pti}"])
                    if kb != qb:
                        mi = 0 if kb < qb else 1
                        P.op("pool", lambda: pool.tensor_tensor(PT[pti][:], PT[pti][:], masks[:, mi, :], op=ALU.mult),
                             r=[f"PT{pti}", "masks"], w=[f"PT{pti}"])
                oi = qb % 2
                for i, kb in enumerate(kbs):
                    P.op("pe", lambda: pe.matmul(p_o[oi][:], v_at[:, kb, g * 128:(g + 1) * 128], PT[pts[i]][:],
                                                 start=(i == 0), stop=(i == len(kbs) - 1)),
                         r=["v_at", f"PT{pts[i]}"], w=[f"p_o{oi}"], inc=(i == len(kbs) - 1))
                for i, kb in enumerate(kbs):
                    P.op("pe", lambda: pe.matmul(p_d[oi][:], ones[:], PT[pts[i]][:],
                                                 start=(i == 0), stop=(i == len(kbs) - 1)),
                         r=["ones", f"PT{pts[i]}"], w=[f"p_d{oi}"], inc=(i == len(kbs) - 1))
                P.op("dve", lambda: dve.tensor_tensor(den[oi][:], p_d[oi][:], esink[:], op=ALU.add),
                     r=[f"p_d{oi}", "esink"], w=[f"den{oi}"])
                P.op("dve", lambda: dve.reciprocal(den[oi][:], den[oi][:]), r=[f"den{oi}"], w=[f"den{oi}"])
                P.op("dve", lambda: dve.tensor_tensor(otmp[oi][:], p_o[oi][:], den[oi][:], op=ALU.mult),
                     r=[f"p_o{oi}", f"den{oi}"], w=[f"otmp{oi}"])
                P.op("pool", lambda: pool.tensor_tensor(yat[:, :, qsl], otmp[oi][:].rearrange("p (h q) -> p h q", h=4),
                                                        zat[:, :, qsl], op=ALU.mult),
                     r=[f"otmp{oi}", "zat"], w=["yat"], nowaw=True)
            P.dma("sp", yatT[g * 4:(g + 1) * 4].rearrange("c p t -> p c t"), yat[:], r=["yat"], w=["yatT"], sem="yat")
        P.phase_end()

    def phase_merge(l):
        P.phase_begin()
        whs = P.sb("whs", [128, 8, D], BF16)
        was = P.sb("was", [128, 8, D], BF16)
        yh = P.sb("yh", [128, 8, L], BF16)
        ya = P.sb("ya", [128, 8, L], BF16)
        for hf in range(2):
            sl = slice(hf * 4, (hf + 1) * 4)
            P.dma("pool", whs[:, sl, :], w_hy[l].rearrange("(c p) n -> p c n", p=128)[:, sl, :], w=["whs"], sem=f"whs{hf}")
            P.dma("pool", was[:, sl, :], w_at[l].rearrange("(c p) n -> p c n", p=128)[:, sl, :], w=["was"], sem=f"was{hf}")
        P.dma("sp", yh[:], yhyT.rearrange("c p t -> p c t"), r=["yhyT"], w=["yh"])
        P.dma("sp", ya[:], yatT.rearrange("c p t -> p c t"), r=["yatT"], w=["ya"])
        sgh = [P.sb(f"sgh{i}", [128, L], BF16) for i in range(2)]
        sga = [P.sb(f"sga{i}", [128, L], BF16) for i in range(2)]
        mrow = [P.sb(f"mrow{i}", [128, L], BF16) for i in range(2)]
        m1 = [P.sb(f"m1_{i}", [128, 512], F32) for i in range(2)]
        m2 = [P.sb(f"m2_{i}", [128, 512], F32) for i in range(2)]
        pa = [P.ps(f"pa{i}", [128, 512], F32) for i in range(2)]
        pb = [P.ps(f"pb{i}", [128, 512], F32) for i in range(2)]
        for Dc in range(16):
            si = Dc % 2
            P.dma("sp", sgh[si][:], sgT[Dc], r=["sgT"], w=[f"sgh{si}"])
            P.dma("sp", sga[si][:], sgT[16 + Dc], r=["sgT"], w=[f"sga{si}"])
            for tq in range(4):
                tsl = slice(tq * 512, (tq + 1) * 512)
                pi = rot(2)
                for cc in range(8):
                    P.op("pe", lambda: pe.matmul(pa[pi][:], whs[:, cc, Dc * 128:(Dc + 1) * 128], yh[:, cc, tsl],
                                                 start=(cc == 0), stop=(cc == 7)),
                         r=["whs", "yh"], w=[f"pa{pi}"], inc=(cc == 7))
                for cc in range(8):
                    P.op("pe", lambda: pe.matmul(pb[pi][:], was[:, cc, Dc * 128:(Dc + 1) * 128], ya[:, cc, tsl],
                                                 start=(cc == 0), stop=(cc == 7)),
                         r=["was", "ya"], w=[f"pb{pi}"], inc=(cc == 7))
                P.op("dve", lambda: dve.tensor_tensor(m1[pi][:], pa[pi][:], sgh[si][:, tsl], op=ALU.mult),
                     r=[f"pa{pi}", f"sgh{si}"], w=[f"m1_{pi}"])
                P.op("dve", lambda: dve.tensor_tensor(m2[pi][:], pb[pi][:], sga[si][:, tsl], op=ALU.mult),
                     r=[f"pb{pi}", f"sga{si}"], w=[f"m2_{pi}"])
                P.op("pool", lambda: pool.tensor_tensor(mrow[si][:, tsl], m1[pi][:], m2[pi][:], op=ALU.add),
                     r=[f"m1_{pi}", f"m2_{pi}"], w=[f"mrow{si}"], nowaw=True)
            P.dma("sp", mrgT[Dc], mrow[si][:], r=[f"mrow{si}"], w=["mrgT"], sem=f"mrow{si}")
        P.phase_end()

    def phase_out(l, xsrc, xdst, final):
        P.phase_begin()
        wo = P.sb("wo", [128, 16, D], BF16)
        wv = w_out[l].rearrange("(c p) n -> p c n", p=128)
        for qd in range(4):
            sl = slice(qd * 4, (qd + 1) * 4)
            P.dma("pool", wo[:, sl, :], wv[:, sl, :], w=["wo"], sem=f"wo{qd}")
        mt = [P.sb(f"mt{i}", [128, 16, 512], BF16) for i in range(2)]
        xt = [P.sb(f"oxt{i}", [128, D], F32) for i in range(2)]
        xo = [P.sb(f"oxo{i}", [128, D], F32) for i in range(2)]
        po = [P.ps(f"po{i}", [128, 512], F32) for i in range(4)]
        if final:
            g_bc = P.sb("gf_bc", [128, D], F32)
            P.dma("sp", g_bc[:], final_norm[0:1, :].partition_broadcast(128), w=["gf_bc"])
            sq = P.sb("osq", [128, D], F32)
            ss = [P.sb(f"oss{i}", [128, 1], F32) for i in range(2)]
            yo = [P.sb(f"oyo{i}", [128, D], F32) for i in range(2)]
        for t4 in range(4):
            mi = t4 % 2
            P.dma("sp", mt[mi][:], mrgT[:, :, t4 * 512:(t4 + 1) * 512].rearrange("c p t -> p c t"), r=["mrgT"], w=[f"mt{mi}"])
            for tt in range(4):
                tb = t4 * 4 + tt
                i = tb % 2
                P.dma("sp", xt[i][:], xsrc[tb * 128:(tb + 1) * 128, :], w=[f"oxt{i}"])
                for nq in range(4):
                    pi = rot(4)
                    nsl = slice(nq * 512, (nq + 1) * 512)
                    for Dc in range(16):
                        P.op("pe", lambda: pe.matmul(po[pi][:], mt[mi][:, Dc, tt * 128:(tt + 1) * 128], wo[:, Dc, nsl],
                                                     start=(Dc == 0), stop=(Dc == 15)),
                             r=[f"mt{mi}", "wo"], w=[f"po{pi}"], inc=(Dc == 15))
                    P.op("dve", lambda: dve.tensor_tensor(xo[i][:, nsl], po[pi][:], xt[i][:, nsl], op=ALU.add),
                         r=[f"po{pi}", f"oxt{i}"], w=[f"oxo{i}"], nowaw=True)
                if not final:
                    P.dma("sp", xdst[tb * 128:(tb + 1) * 128, :], xo[i][:], r=[f"oxo{i}"], w=["xdst"], sem=f"oxo{i}")
                else:
                    P.op("act", lambda: act.activation(sq[:], xo[i][:], AF.Square, accum_out=ss[i][:]),
                         r=[f"oxo{i}"], w=["osq", f"oss{i}"])
                    P.op("dve", lambda: dve.tensor_scalar(ss[i][:], ss[i][:], 1.0 / D, 1e-6, op0=ALU.mult, op1=ALU.add),
                         r=[f"oss{i}"], w=[f"oss{i}"])
                    P.op("act", lambda: act.sqrt(ss[i][:], ss[i][:]), r=[f"oss{i}"], w=[f"oss{i}"])
                    P.op("dve", lambda: dve.reciprocal(ss[i][:], ss[i][:]), r=[f"oss{i}"], w=[f"oss{i}"])
                    P.op("dve", lambda: dve.scalar_tensor_tensor(yo[i][:], xo[i][:], ss[i][:, 0:1], g_bc[:],
                                                                 op0=ALU.mult, op1=ALU.mult),
                         r=[f"oxo{i}", f"oss{i}", "gf_bc"], w=[f"oyo{i}"])
                    P.dma("sp", xdst[tb * 128:(tb + 1) * 128, :], yo[i][:], r=[f"oyo{i}"], w=["xdst"], sem=f"oyo{i}")
        P.phase_end()

    stop = [d for d in dbg if d.startswith("stop_")]
    stop = stop[0] if stop else None
    xsrc = x_in
    for l in range(nlayers):
        last = (l == nlayers - 1)
        phase_filter(l)
        if stop == "stop_filter":
            break
        P.phase_begin()
        hold["vx_tok"] = P.sb("vx_tok", [128, 2, 16, C], BF16)
        hold["v_at"] = P.sb("v_at", [128, 16, 256], BF16)
        phase_proj(l, xsrc)
        if stop == "stop_proj":
            break
        phase_hyena(l)
        if stop == "stop_hyena":
            break
        phase_attn(l)
        P.phase_end()
        if stop == "stop_attn":
            break
        phase_merge(l)
        if stop == "stop_merge":
            break
        phase_out(l, xsrc, out if last else xa, last)
        xsrc = xa
    P.finish()
    return nc, P


_SMALL = ("norm_g", "filt_w1", "filt_w2", "filt_w3", "filt_w4", "w_in", "w_hyena_out", "w_attn_out", "w_out")


def host_inputs(inputs):
    f32 = np.float32
    c = make_consts()
    shared = {}
    for k in _SMALL:
        shared[k] = np.ascontiguousarray(inputs[k], dtype=f32)
    shared["final_norm"] = np.ascontiguousarray(inputs["final_norm"], dtype=f32).reshape(1, D)
    cw = np.asarray(inputs["conv_w"], dtype=f32)
    shared["conv_wp"] = np.ascontiguousarray(cw.reshape(2, 3, 24, 128).transpose(0, 3, 1, 2))
    cb = np.asarray(inputs["conv_b"], dtype=f32)
    shared["conv_b"] = np.ascontiguousarray(cb)
    shared["conv_bp"] = np.ascontiguousarray(cb.reshape(2, 24, 128).transpose(0, 2, 1))
    shared["filt_sc"] = np.ascontiguousarray(np.stack(
        [np.asarray(inputs[k], dtype=f32) for k in ("filt_freq", "filt_b1", "filt_b2", "filt_b3")], axis=-1))
    shared["hyena_bias"] = np.ascontiguousarray(np.asarray(inputs["hyena_bias"], dtype=f32).reshape(2, 2 * C))
    sk = np.asarray(inputs["attn_sink"], dtype=f32)
    shared["sinkx"] = np.ascontiguousarray(np.repeat(sk.reshape(2, 2, 4, 1), 128, axis=3).reshape(2, 2, 512))
    for k in ("FT", "GT", "featsT", "decay", "ropeC", "ropeS", "RT", "masks"):
        shared[k] = c[k]
    x = np.asarray(inputs["x"], dtype=f32)
    return [dict(shared, x=np.ascontiguousarray(x[b])) for b in range(8)]


_NC = None


def kernel(**inputs):
    global _NC
    if _NC is None:
        _NC = build()[0]
    in_maps = host_inputs(inputs)
    res = run_bass_kernel_spmd(_NC, in_maps, core_ids=list(range(8)))
    return np.stack([np.asarray(r["out"], dtype=np.float32) for r in res.results], axis=0)
```

```python
import contextlib
import numpy as np
import ml_dtypes
import concourse.bass as bass
import concourse.mybir as mybir
from concourse.bass_utils import run_bass_kernel_spmd

F32 = mybir.dt.float32
BF16 = mybir.dt.bfloat16
AF = mybir.ActivationFunctionType
ALU = mybir.AluOpType
AX = mybir.AxisListType


class Prog:
    CE = ("pe", "act", "dve", "pool")

    def __init__(self, nc):
        self.nc = nc
        self.eng = {"pe": nc.tensor, "act": nc.scalar, "dve": nc.vector,
                    "pool": nc.gpsimd, "sp": nc.sync}
        self.csem = {e: nc.alloc_semaphore("s_" + e) for e in self.CE}
        self.cnt = {e: 0 for e in self.CE}
        self.pending = {e: False for e in self.CE}
        self.seen = {}
        self.kw = {}
        self.kr = {}
        self.dsem = {}
        self.dlast = {}
        self.stack = contextlib.ExitStack()
        self.phase_stack = None
        self.nops = 0

    def sb(self, name, shape, dtype, perm=False):
        st = self.stack if (perm or self.phase_stack is None) else self.phase_stack
        self.uid = getattr(self, "uid", 0) + 1
        return st.enter_context(self.nc.sbuf_tensor(f"sb{self.uid}_{name}", list(shape), dtype))

    def ps(self, name, shape, dtype=F32, perm=False):
        st = self.stack if (perm or self.phase_stack is None) else self.phase_stack
        self.uid = getattr(self, "uid", 0) + 1
        return st.enter_context(self.nc.psum_tensor(f"ps{self.uid}_{name}", list(shape), dtype))

    def phase_begin(self):
        self.scopes = getattr(self, "scopes", [])
        self.scopes.append(self.phase_stack)
        self.phase_stack = contextlib.ExitStack()

    def phase_end(self):
        self.barrier()
        self.phase_stack.close()
        self.phase_stack = self.scopes.pop()

    def _wait(self, F, tok):
        semname, sem, val, E = tok
        if E == "pe" and F == "pe":
            return
        k = (F, semname)
        if self.seen.get(k, 0) >= val:
            return
        self.eng[F].wait_ge(sem, val)
        self.seen[k] = val

    def _deps(self, F, r, w, nowaw):
        for k in r:
            for tok in self.kw.get(k, {}).values():
                self._wait(F, tok)
        for k in w:
            for tok in self.kr.get(k, {}).values():
                if tok[3] == F and F is not None and F in self.CE:
                    continue
                self._wait(F, tok)
            if not nowaw:
                for tok in self.kw.get(k, {}).values():
                    if tok[3] == F and F in self.CE:
                        continue
                    self._wait(F, tok)

    def _record(self, tok, r, w, nowaw):
        semname = tok[0]
        for k in w:
            if nowaw:
                self.kw.setdefault(k, {})[semname] = tok
            else:
                self.kw[k] = {semname: tok}
            self.kr[k] = {}
        for k in r:
            self.kr.setdefault(k, {})[semname] = tok

    def op(self, F, fn, r=(), w=(), nowaw=False, inc=True):
        self._flush(r, w)
        self._deps(F, r, w, nowaw)
        ins = fn()
        sem = self.csem[F]
        if inc:
            self.cnt[F] += 1
            ins.then_inc(sem, 1)
            self.pending[F] = False
            val = self.cnt[F]
        else:
            self.pending[F] = True
            val = self.cnt[F] + 1
        self._record(("s_" + F, sem, val, F), r, w, nowaw)
        self.nops += 1
        return ins

    def _flush(self, r=(), w=(), force=False):
        pend = getattr(self, "pend", None)
        if not pend:
            return
        if not force:
            rs, ws = set(r), set(w)
            hit = False
            for (_, _, _, pr, pw, _) in pend:
                if ws & (set(pr) | set(pw)) or rs & set(pw):
                    hit = True
                    break
            if not hit and len(pend) < 6:
                return
        self.pend = []
        for (Q, out, in_, pr, pw, sem) in pend:
            self.dma(Q, out, in_, r=pr, w=pw, sem=sem, _noflush=True)

    def store(self, Q, out, in_, r=(), w=(), sem=None):
        self._flush(r, w)
        self.pend = getattr(self, "pend", [])
        self.pend.append((Q, out, in_, tuple(r), tuple(w), sem))

    def dma(self, Q, out, in_, r=(), w=(), sem=None, nowaw=True, _noflush=False, **kw):
        if not _noflush:
            self._flush(r, w)
        semname = "d_" + (sem if sem is not None else (w[0] if w else r[0]))
        if semname not in self.dsem:
            self.dsem[semname] = [self.nc.alloc_semaphore(semname), 0]
        ent = self.dsem[semname]
        if semname in self.dlast:
            s, v = self.dlast[semname]
            self._wait(Q, (semname, s, v, None))
        self._deps(Q, r, w, nowaw)
        ins = self.eng[Q].dma_start(out=out, in_=in_, **kw)
        ent[1] += 16
        ins.then_inc(ent[0], 16)
        tok = (semname, ent[0], ent[1], None)
        self.dlast[semname] = (ent[0], ent[1])
        self._record(tok, r, w, nowaw)
        self.nops += 1
        return ins

    def barrier(self, engines=("pe", "act", "dve", "pool", "sp")):
        self._flush(force=True)
        for e in self.CE:
            assert not self.pending[e], e
        for F in engines:
            for E in self.CE:
                if self.cnt[E] > 0:
                    self._wait(F, ("s_" + E, self.csem[E], self.cnt[E], None))
            for semname, (s, v) in self.dlast.items():
                self._wait(F, (semname, s, v, None))

    def finish(self):
        self.barrier()
        while self.phase_stack is not None:
            self.phase_stack.close()
            self.phase_stack = self.scopes.pop() if getattr(self, "scopes", None) else None
        self.stack.close()


L = 2048
D = 2048
C = 1024
NW = 10752
MAGIC = 12582912.0
TWO_PI = 2.0 * np.pi

_CONSTS = None


def make_consts():
    global _CONSTS
    if _CONSTS is not None:
        return _CONSTS
    bf = ml_dtypes.bfloat16
    s = np.arange(L, dtype=np.float64)
    w = np.pi * (2 * np.arange(L, dtype=np.float64) + 1) / (2 * L)
    ang = np.outer(s, w)
    Cm = np.cos(ang)
    Sm = np.sin(ang)
    M = np.empty((L, 32, 128))
    M[:, 0::2] = Cm.reshape(L, 16, 128)
    M[:, 1::2] = -Sm.reshape(L, 16, 128)
    FT = np.ascontiguousarray(M.reshape(16, 128, 32, 128).transpose(2, 1, 0, 3)).astype(bf)
    G = np.empty((32, 128, L))
    G[0::2] = Cm.T.reshape(16, 128, L) * (2.0 / (2 * L))
    G[1::2] = -Sm.T.reshape(16, 128, L) * (2.0 / (2 * L))
    GT = np.ascontiguousarray(G.reshape(32, 128, 16, 128).transpose(2, 1, 0, 3)).astype(bf)
    del M, G, Cm, Sm, ang
    f32 = np.float32
    t = np.linspace(0.0, 1.0, L, dtype=f32)[:, None]
    bands = np.linspace(1e-4, 15.0, 16, dtype=f32)[None, :]
    a = (f32(2.0 * np.pi / L) * np.arange(L, dtype=f32)[:, None]) * bands
    feats = np.concatenate([t, np.cos(a), -np.sin(a)], axis=-1).astype(f32)
    featsT = np.ascontiguousarray(feats.T)
    mind = np.log(1e-2) / 0.3
    maxd = np.log(1e-2) / 1.5
    deltas = np.abs(np.linspace(mind, maxd, C, dtype=f32))
    dec = np.exp(-t * deltas[None, :]).astype(f32)
    dec_b = dec.copy()
    dec_b[0] = 0.0
    decay = np.stack([dec, dec_b], axis=1)
    decay = np.ascontiguousarray(decay)
    inv = (500000.0 ** (-np.arange(0, 32, 2, dtype=f32) / f32(32))).astype(f32)
    ra = np.arange(L, dtype=f32)[None, :] * inv[:, None]
    ropeC = np.ones((128, L), f32)
    ropeS = np.zeros((128, L), f32)
    ropeC[0:16] = np.cos(ra)
    ropeC[16:32] = np.cos(ra)
    ropeS[0:16] = np.sin(ra)
    ropeS[16:32] = np.sin(ra)
    RT = np.zeros((128, 128), f32)
    for i in range(16):
        RT[i + 16, i] = -1.0
        RT[i, i + 16] = 1.0
    kl = np.arange(128)[:, None]
    ql = np.arange(128)[None, :]
    m0 = (kl >= ql).astype(f32)
    m1 = (kl <= ql).astype(f32)
    masks = np.stack([np.tile(m0, (1, 4)), np.tile(m1, (1, 4))], axis=1)
    _CONSTS = dict(FT=FT, GT=GT, featsT=featsT, decay=decay, ropeC=ropeC, ropeS=ropeS,
                   RT=RT.astype(bf), masks=np.ascontiguousarray(masks).astype(bf))
    return _CONSTS


def build(nlayers=2, dbg=()):
    nc = bass.Bass("TRN2", target_bir_lowering=False)
    P = Prog(nc)
    act, dve, pool, pe = nc.scalar, nc.vector, nc.gpsimd, nc.tensor

    def din(name, shape, dt=F32):
        return nc.dram_tensor(name, list(shape), dt, kind="ExternalInput").ap()

    def dscr(name, shape, dt):
        kind = "ExternalOutput" if name in dbg else "Internal"
        return nc.dram_tensor(name, list(shape), dt, kind=kind).ap()

    x_in = din("x", [L, D])
    norm_g = din("norm_g", [2, D])
    final_norm = din("final_norm", [1, D])
    w_in = din("w_in", [2, D, NW])
    conv_wp = din("conv_wp", [2, 128, 3, 24])
    conv_b = din("conv_b", [2, 3 * C])
    conv_bp = din("conv_bp", [2, 128, 24])
    fw1 = din("filt_w1", [2, 33, 64])
    fw2 = din("filt_w2", [2, 64, 64])
    fw3 = din("filt_w3", [2, 64, 64])
    fw4 = din("filt_w4", [2, 64, 4 * C])
    fsc = din("filt_sc", [2, 64, 4])
    hbias = din("hyena_bias", [2, 2 * C])
    sinkx = din("sinkx", [2, 2, 512])
    w_hy = din("w_hyena_out", [2, C, D])
    w_at = din("w_attn_out", [2, C, D])
    w_out = din("w_out", [2, D, D])
    FT = din("FT", [32, 128, 16, 128], BF16)
    GT = din("GT", [16, 128, 32, 128], BF16)
    featsT = din("featsT", [33, L])
    decay = din("decay", [L, 2, C])
    ropeC = din("ropeC", [128, L])
    ropeS = din("ropeS", [128, L])
    RTd = din("RT", [128, 128], BF16)
    masksd = din("masks", [128, 2, 512], BF16)

    out = nc.dram_tensor("out", [L, D], F32, kind="ExternalOutput").ap()
    Kf = dscr("Kf", [2, 32, 128, C], F32)
    x2T = dscr("x2T", [8, 128, L], BF16)
    zhyT = dscr("zhyT", [8, 128, L], BF16)
    qT = dscr("qT", [8, 128, L], BF16)
    kT = dscr("kT", [2, 128, L], BF16)
    zatT = dscr("zatT", [8, 128, L], BF16)
    sgT = dscr("sgT", [32, 128, L], BF16)
    yhyT = dscr("yhyT", [8, 128, L], BF16)
    yatT = dscr("yatT", [8, 128, L], BF16)
    mrgT = dscr("mrgT", [16, 128, L], BF16)
    xa = dscr("xa", [L, D], F32)
    dbg_vx = dscr("dbg_vx", [128, 2, 16, C], BF16) if "dbg_vx" in dbg else None
    dbg_vat = dscr("dbg_vat", [128, 16, 256], BF16) if "dbg_vat" in dbg else None

    ident = P.sb("ident", [128, 128], BF16, perm=True)
    ones = P.sb("ones", [128, 128], BF16, perm=True)
    hold = {}
    P.op("pool", lambda: pool.memset(ident[:], 1.0), w=["ident"])
    P.op("pool", lambda: pool.affine_select(ident[:], ident[:], pattern=[[-1, 128]],
                                            compare_op=ALU.is_equal, fill=0.0, base=0,
                                            channel_multiplier=1), r=["ident"], w=["ident"])
    P.op("pool", lambda: pool.memset(ones[:], 1.0), w=["ones"])

    rr = {"n": 0}

    def rot(n):
        rr["n"] += 1
        return rr["n"] % n

    def sin_layer(lhsT_ap, rhs_fn, freq_ap, fb_ap, ag, md, hout, pss):
        for q in range(4):
            ps = pss[q % 2]
            psk = f"fps{q % 2}"
            P.op("pe", lambda: pe.matmul(ps[:], lhsT_ap, rhs_fn(q), start=True, stop=True),
                 r=["fw", "fh"], w=[psk])
            P.op("dve", lambda: dve.tensor_scalar(ag[:, q * 512:(q + 1) * 512], ps[:], freq_ap, fb_ap,
                                                  op0=ALU.mult, op1=ALU.add), r=[psk, "fsc"], w=["ag"], nowaw=True)
        P.op("dve", lambda: dve.tensor_scalar(md[:], ag[:], 1.0 / TWO_PI, MAGIC, op0=ALU.mult, op1=ALU.add),
             r=["ag"], w=["md"])
        P.op("dve", lambda: dve.tensor_scalar(md[:], md[:], MAGIC, None, op0=ALU.subtract), r=["md"], w=["md"])
        P.op("dve", lambda: dve.scalar_tensor_tensor(md[:], md[:], -TWO_PI, ag[:], op0=ALU.mult, op1=ALU.add),
             r=["md", "ag"], w=["md"])
        P.op("act", lambda: act.activation(hout[:], md[:], AF.Sin, scale=0.999999), r=["md"], w=["fh"])

    def phase_filter(l):
        P.phase_begin()
        fT = P.sb("featsT", [33, L], F32)
        w1 = P.sb("fw1", [33, 64], F32)
        w2 = P.sb("fw2", [64, 64], F32)
        w3 = P.sb("fw3", [64, 64], F32)
        w4 = P.sb("fw4", [64, 4 * C], F32)
        sc = P.sb("fsc", [64, 4], F32)
        fb = P.sb("ffb", [64, 3], F32)
        ag = P.sb("fag", [64, L], F32)
        md = P.sb("fmd", [64, L], F32)
        hA = P.sb("fhA", [64, L], F32)
        hB = P.sb("fhB", [64, L], F32)
        hb_bc = P.sb("hb_bc", [128, 2 * C], F32)
        ksum = P.sb("ksum", [128, 16, C], BF16)
        kdif = P.sb("kdif", [128, 16, C], BF16)
        pss = [P.ps(f"fps{i}", [128, 512], F32) for i in range(4)]
        P.dma("sp", fT[:], featsT, w=["fh"], sem="f_fT")
        P.dma("sp", w1[:], fw1[l], w=["fw"], sem="f_w1")
        P.dma("sp", w2[:], fw2[l], w=["fw"], sem="f_w2")
        P.dma("sp", w3[:], fw3[l], w=["fw"], sem="f_w3")
        P.dma("sp", w4[:], fw4[l], w=["fw"], sem="f_w4")
        P.dma("sp", sc[:], fsc[l], w=["fsc"], sem="f_sc")
        P.dma("sp", hb_bc[:], hbias[l:l + 1, :].partition_broadcast(128), w=["hb_bc"])
        P.op("dve", lambda: dve.tensor_scalar(fb[:], sc[:, 1:4], sc[:, 0:1], None, op0=ALU.mult),
             r=["fsc"], w=["fsc2"])
        P.kw["fsc"].update(P.kw["fsc2"])
        ps64 = [p_[0:64, :] for p_ in pss]
        sin_layer(w1[:], lambda q: fT[:, q * 512:(q + 1) * 512], sc[:, 0:1], fb[:, 0:1], ag, md, hA, ps64)
        sin_layer(w2[:], lambda q: hA[:, q * 512:(q + 1) * 512], sc[:, 0:1], fb[:, 1:2], ag, md, hB, ps64)
        sin_layer(w3[:], lambda q: hB[:, q * 512:(q + 1) * 512], sc[:, 0:1], fb[:, 2:3], ag, md, hA, ps64)
        dct = [P.sb(f"dct{i}", [128, 2, C], F32) for i in range(2)]
        tf = [P.sb(f"ftf{i}", [128, 512], F32) for i in range(2)]
        tb_ = [P.sb(f"ftb{i}", [128, 512], F32) for i in range(2)]
        fts = [P.sb(f"fts{i}", [128, 16, 128], BF16) for i in range(3)]
        kst = [P.sb(f"kst{i}", [128, C], F32) for i in range(2)]
        nd = 0
        nf = 0
        for o in range(2):
            for sb in range(16):
                di = nd % 2
                nd += 1
                P.dma("sp", dct[di][:], decay[sb * 128:(sb + 1) * 128], w=[f"dct{di}"])
                for half in range(2):
                    i = rot(2)
                    c0 = half * 512
                    for dr in range(2):
                        n0 = o * 2048 + dr * 1024 + c0
                        ps = pss[2 * i + dr]
                        P.op("pe", lambda: pe.matmul(ps[:], hA[:, sb * 128:(sb + 1) * 128], w4[:, n0:n0 + 512],
                                                     start=True, stop=True), r=["fh", "fw"], w=[f"fps{2 * i + dr}"])
                        tt = (tf, tb_)[dr][i]
                        P.op("dve", lambda: dve.tensor_tensor(tt[:], ps[:], dct[di][:, dr, c0:c0 + 512], op=ALU.mult),
                             r=[f"fps{2 * i + dr}", f"dct{di}"], w=[f"ft{dr}{i}"])
                    P.op("pool", lambda: pool.tensor_tensor(ksum[:, sb, c0:c0 + 512], tf[i][:], tb_[i][:],
                                                            op=ALU.add), r=[f"ft0{i}", f"ft1{i}"], w=["ksum"], nowaw=True)
                    P.op("pool", lambda: pool.tensor_tensor(kdif[:, sb, c0:c0 + 512], tf[i][:], tb_[i][:],
                                                            op=ALU.subtract), r=[f"ft0{i}", f"ft1{i}"], w=["kdif"], nowaw=True)
            for frb in range(32):
                fi = nf % 3
                nf += 1
                P.dma("sp", fts[fi][:], FT[frb], w=[f"fts{fi}"])
                src = ksum if frb % 2 == 0 else kdif
                srck = "ksum" if frb % 2 == 0 else "kdif"
                ki = frb % 2
                for nq in range(2):
                    pi = rot(4)
                    ps = pss[pi]
                    for sc_ in range(16):
                        P.op("pe", lambda: pe.matmul(ps[:], fts[fi][:, sc_, :], src[:, sc_, nq * 512:(nq + 1) * 512],
                                                     start=(sc_ == 0), stop=(sc_ == 15)),
                             r=[f"fts{fi}", srck], w=[f"fps{pi}"], inc=(sc_ == 15))
                    if frb % 2 == 0:
                        P.op("dve", lambda: dve.tensor_tensor(kst[ki][:, nq * 512:(nq + 1) * 512], ps[:],
                                                              hb_bc[:, o * C + nq * 512:o * C + (nq + 1) * 512], op=ALU.add),
                             r=[f"fps{pi}", "hb_bc"], w=[f"kst{ki}"], nowaw=True)
                    else:
                        P.op("act", lambda: act.activation(kst[ki][:, nq * 512:(nq + 1) * 512], ps[:], AF.Copy),
                             r=[f"fps{pi}"], w=[f"kst{ki}"], nowaw=True)
                P.store("sp", Kf[o, frb], kst[ki][:], r=[f"kst{ki}"], w=["Kf"], sem=f"kst{ki}")
        P.phase_end()

    def phase_proj(l, xsrc):
        vx_tok, v_at = hold["vx_tok"], hold["v_at"]
        P.phase_begin()
        hT = P.sb("hT", [128, 16, L], BF16)
        P.phase_begin()
        g_bc = P.sb("g_bc", [128, D], F32)
        P.dma("sp", g_bc[:], norm_g[l:l + 1, :].partition_broadcast(128), w=["g_bc"])
        xt = [P.sb(f"xt{i}", [128, D], F32) for i in range(2)]
        sq = P.sb("sq", [128, D], F32)
        ss = [P.sb(f"ss{i}", [128, 1], F32) for i in range(2)]
        xn = [P.sb(f"xn{i}", [128, D], BF16) for i in range(2)]
        pt = [P.ps(f"pt{i}", [128, 4, 128], BF16) for i in range(2)]
        for tb in range(16):
            i = tb % 2
            P.dma("sp", xt[i][:], xsrc[tb * 128:(tb + 1) * 128, :], w=[f"xt{i}"])
            P.op("act", lambda: act.activation(sq[:], xt[i][:], AF.Square, accum_out=ss[i][:]),
                 r=[f"xt{i}"], w=["sq", f"ss{i}"])
            P.op("dve", lambda: dve.tensor_scalar(ss[i][:], ss[i][:], 1.0 / D, 1e-6, op0=ALU.mult, op1=ALU.add),
                 r=[f"ss{i}"], w=[f"ss{i}"])
            P.op("act", lambda: act.sqrt(ss[i][:], ss[i][:]), r=[f"ss{i}"], w=[f"ss{i}"])
            P.op("dve", lambda: dve.reciprocal(ss[i][:], ss[i][:]), r=[f"ss{i}"], w=[f"ss{i}"])
            P.op("dve", lambda: dve.scalar_tensor_tensor(xn[i][:], xt[i][:], ss[i][:, 0:1], g_bc[:],
                                                         op0=ALU.mult, op1=ALU.mult),
                 r=[f"xt{i}", f"ss{i}", "g_bc"], w=[f"xn{i}"])
            for q in range(4):
                j = rot(2)
                for c in range(4):
                    dc = q * 4 + c
                    P.op("pe", lambda: pe.transpose(pt[j][:, c, :], xn[i][:, dc * 128:(dc + 1) * 128], ident[:]),
                         r=[f"xn{i}", "ident"], w=[f"pt{j}"], inc=(c == 3))
                P.op("act", lambda: act.activation(hT[:, q * 4:(q + 1) * 4, tb * 128:(tb + 1) * 128], pt[j][:], AF.Copy),
                     r=[f"pt{j}"], w=["hT"], nowaw=True)

        P.phase_end()
        wsl = [P.sb(f"wsl{i}", [128, 16, 256], BF16) for i in range(3)]
        upad = [P.sb(f"upad{i}", [128, L + 2], BF16) for i in range(2)]
        stg = [P.sb(f"stg{i}", [128, L], BF16) for i in range(3)]
        dg = [[P.sb(f"dg{i}_{j}", [128, 128], BF16) for j in range(3)] for i in range(2)]
        cw = P.sb("cw", [128, 3, 24], F32)
        cbp = P.sb("cbp", [128, 24], F32)
        cbrow = P.sb("cbrow", [1, 3 * C], BF16)
        P.dma("sp", cw[:], conv_wp[l], w=["cw"])
        P.dma("sp", cbp[:], conv_bp[l], w=["cbp"])
        P.dma("pool", cbrow[:], conv_b[l:l + 1, :], w=["cbrow"])
        for i in range(2):
            P.op("pool", lambda: pool.memset(upad[i][:, 0:1], 0.0), w=[f"upad{i}"], nowaw=True)
            P.op("pool", lambda: pool.memset(upad[i][:, L + 1:L + 2], 0.0), w=[f"upad{i}"], nowaw=True)
        pm = [P.ps(f"pm{i}", [128, 512], F32) for i in range(4)]
        pc = [P.ps(f"pc{i}", [128, 4, 128], F32) for i in range(2)]
        w_l = w_in[l].rearrange("(c p) n -> p c n", p=128)

        def load_slab(si):
            b = si % 3
            P.dma("pool", wsl[b][:], w_l[:, :, si * 256:(si + 1) * 256], w=[f"wsl{b}"])

        def fm_matmul(b, j, tq):
            pi = rot(4)
            for dc in range(16):
                P.op("pe", lambda: pe.matmul(pm[pi][:], wsl[b][:, dc, j * 128:(j + 1) * 128],
                                             hT[:, dc, tq * 512:(tq + 1) * 512],
                                             start=(dc == 0), stop=(dc == 15)),
                     r=[f"wsl{b}", "hT"], w=[f"pm{pi}"], inc=(dc == 15))
            return pi

        load_slab(0)
        load_slab(1)
        stg_n = 0
        for si in range(42):
            if si + 2 < 42:
                load_slab(si + 2)
            b = si % 3
            for j in range(2):
                n = 2 * si + j
                if n < 24:
                    ui = n % 2
                    for tq in range(4):
                        pi = fm_matmul(b, j, tq)
                        P.op("act", lambda: act.activation(upad[ui][:, 1 + tq * 512:1 + (tq + 1) * 512], pm[pi][:], AF.Copy),
                             r=[f"pm{pi}"], w=[f"upad{ui}"], nowaw=True)
                    for tap in range(3):
                        P.op("dve", lambda: dve.tensor_scalar(dg[ui][tap][:], ident[:], cw[:, tap, n:n + 1], None,
                                                              op0=ALU.mult), r=["ident", "cw"], w=[f"dg{ui}"], nowaw=True)
                    stream, cc = n // 8, n % 8
                    if stream < 2:
                        for t4 in range(4):
                            ci = rot(2)
                            for tt in range(4):
                                tb = t4 * 4 + tt
                                for tap in range(3):
                                    P.op("pe", lambda: pe.matmul(pc[ci][:, tt, :], upad[ui][:, tb * 128 + tap:tb * 128 + tap + 128],
                                                                 dg[ui][tap][:], start=(tap == 0), stop=False),
                                         r=[f"upad{ui}", f"dg{ui}"], w=[f"pc{ci}"], inc=False)
                                P.op("pe", lambda: pe.matmul(pc[ci][:, tt, :], ones[0:1, :], cbrow[0:1, n * 128:(n + 1) * 128],
                                                             start=False, stop=True),
                                     r=["ones", "cbrow"], w=[f"pc{ci}"], inc=(tt == 3))
                            P.op("act", lambda: act.activation(vx_tok[:, stream, t4 * 4:(t4 + 1) * 4, cc * 128:(cc + 1) * 128],
                                                               pc[ci][:], AF.Copy),
                                 r=[f"pc{ci}"], w=["vx_tok"], nowaw=True)
                    else:
                        sgi = stg_n % 3
                        stg_n += 1
                        for tq in range(4):
                            pi = rot(4)
                            for tap in range(3):
                                P.op("pe", lambda: pe.matmul(pm[pi][:], dg[ui][tap][:],
                                                             upad[ui][:, tq * 512 + tap:tq * 512 + tap + 512],
                                                             start=(tap == 0), stop=(tap == 2)),
                                     r=[f"upad{ui}", f"dg{ui}"], w=[f"pm{pi}"], inc=(tap == 2))
                            P.op("act", lambda: act.activation(stg[sgi][:, tq * 512:(tq + 1) * 512], pm[pi][:], AF.Identity,
                                                               bias=cbp[:, n:n + 1]),
                                 r=[f"pm{pi}", "cbp"], w=[f"stg{sgi}"], nowaw=True)
                        P.store("sp", x2T[cc], stg[sgi][:], r=[f"stg{sgi}"], w=["x2T"], sem=f"stg{sgi}")
                elif n in (42, 43):
                    for t4 in range(4):
                        ci = rot(2)
                        for tt in range(4):
                            tb = t4 * 4 + tt
                            for dc in range(16):
                                P.op("pe", lambda: pe.matmul(pc[ci][:, tt, :], hT[:, dc, tb * 128:(tb + 1) * 128],
                                                             wsl[b][:, dc, j * 128:(j + 1) * 128],
                                                             start=(dc == 0), stop=(dc == 15)),
                                     r=[f"wsl{b}", "hT"], w=[f"pc{ci}"], inc=(dc == 15 and tt == 3))
                        P.op("act", lambda: act.activation(v_at[:, t4 * 4:(t4 + 1) * 4, (n - 42) * 128:(n - 41) * 128],
                                                           pc[ci][:], AF.Copy),
                             r=[f"pc{ci}"], w=["v_at"], nowaw=True)
                else:
                    if n < 32:
                        fn, dst, dk = AF.Silu, zhyT[n - 24], "zhyT"
                    elif n < 40:
                        fn, dst, dk = AF.Copy, qT[n - 32], "qT"
                    elif n < 42:
                        fn, dst, dk = AF.Copy, kT[n - 40], "kT"
                    elif n < 52:
                        fn, dst, dk = AF.Silu, zatT[n - 44], "zatT"
                    else:
                        fn, dst, dk = AF.Sigmoid, sgT[n - 52], "sgT"
                    sgi = stg_n % 3
                    stg_n += 1
                    for tq in range(4):
                        pi = fm_matmul(b, j, tq)
                        P.op("act", lambda: act.activation(stg[sgi][:, tq * 512:(tq + 1) * 512], pm[pi][:], fn),
                             r=[f"pm{pi}"], w=[f"stg{sgi}"], nowaw=True)
                    P.store("sp", dst, stg[sgi][:], r=[f"stg{sgi}"], w=[dk], sem=f"stg{sgi}")
        if dbg_vx is not None:
            P.dma("sp", dbg_vx, vx_tok[:], r=["vx_tok"], w=["dbg_vx"], sem="dbgvx")
        if dbg_vat is not None:
            P.dma("sp", dbg_vat, v_at[:], r=["v_at"], w=["dbg_vat"], sem="dbgvat")
        P.phase_end()

    def phase_hyena(l):
        vx_tok = hold["vx_tok"]
        P.phase_begin()
        Y = P.sb("Y", [128, 32, 512], BF16)
        z_tok = P.sb("z_tok", [128, 16, 512], BF16)
        fts = [P.sb(f"hfts{i}", [128, 16, 128], BF16) for i in range(3)]
        gts = [P.sb(f"gts{i}", [128, 32, 128], BF16) for i in range(2)]
        gt2 = [P.sb(f"gt2_{i}", [128, 32, 128], BF16) for i in range(2)]
        kri = [P.sb(f"kri{i}", [128, 2, 512], F32) for i in range(2)]
        tmp = [[P.sb(f"htmp{i}_{j}", [128, 512], F32) for j in range(4)] for i in range(2)]
        x2t = [P.sb(f"x2t{i}", [128, 4, 128], BF16) for i in range(2)]
        zht = [P.sb(f"zht{i}", [128, 4, 128], BF16) for i in range(2)]
        yt = [P.sb(f"yt{i}", [128, 4, 128], BF16) for i in range(2)]
        ytmp = [P.sb(f"ytmp{i}", [128, 128], F32) for i in range(2)]
        pu = [P.ps(f"pu{i}", [128, 512], F32) for i in range(4)]
        pz = [P.ps(f"pz{i}", [128, 512], F32) for i in range(2)]
        py = [P.ps(f"py{i}", [128, 128], F32) for i in range(2)]
        nft = {"n": 0}

        def dft(src_fn, srck, o, c0):
            for fb in range(16):
                ki = fb % 2
                P.dma("sp", kri[ki][:], Kf[o, 2 * fb:2 * fb + 2, :, c0:c0 + 512].rearrange("k p c -> p k c"),
                      r=["Kf"], w=[f"kri{ki}"])
                pis = []
                for kind in range(2):
                    frb = 2 * fb + kind
                    fi = nft["n"] % 3
                    nft["n"] += 1
                    P.dma("sp", fts[fi][:], FT[frb], w=[f"hfts{fi}"])
                    pi = 2 * (fb % 2) + kind
                    pis.append(pi)
                    for sc_ in range(16):
                        P.op("pe", lambda: pe.matmul(pu[pi][:], fts[fi][:, sc_, :], src_fn(sc_),
                                                     start=(sc_ == 0), stop=(sc_ == 15)),
                             r=[f"hfts{fi}", srck], w=[f"pu{pi}"], inc=(sc_ == 15))
                ur, ui = pu[pis[0]], pu[pis[1]]
                urk, uik = f"pu{pis[0]}", f"pu{pis[1]}"
                t = tmp[ki]
                tk = f"htmp{ki}"
                kr_, ki_ = kri[ki][:, 0, :], kri[ki][:, 1, :]
                P.op("dve", lambda: dve.tensor_tensor(t[0][:], ur[:], kr_, op=ALU.mult), r=[urk, f"kri{ki}"], w=[tk + "a"])
                P.op("dve", lambda: dve.tensor_tensor(t[1][:], ui[:], ki_, op=ALU.mult), r=[uik, f"kri{ki}"], w=[tk + "b"])
                P.op("dve", lambda: dve.tensor_tensor(t[2][:], ur[:], ki_, op=ALU.mult), r=[urk, f"kri{ki}"], w=[tk + "c"])
                P.op("dve", lambda: dve.tensor_tensor(t[3][:], ui[:], kr_, op=ALU.mult), r=[uik, f"kri{ki}"], w=[tk + "d"])
                P.op("pool", lambda: pool.tensor_tensor(Y[:, 2 * fb, :], t[0][:], t[1][:], op=ALU.subtract),
                     r=[tk + "a", tk + "b"], w=["Y"], nowaw=True)
                P.op("pool", lambda: pool.tensor_tensor(Y[:, 2 * fb + 1, :], t[2][:], t[3][:], op=ALU.add),
                     r=[tk + "c", tk + "d"], w=["Y"], nowaw=True)

        for g in range(2):
            c0 = g * 512
            dft(lambda sc_: vx_tok[:, 0, sc_, c0:c0 + 512], "vx_tok", 0, c0)
            for tb in range(16):
                gi = tb % 2
                P.dma("sp", gts[gi][:], GT[tb], w=[f"gts{gi}"])
                for frc in range(32):
                    P.op("pe", lambda: pe.matmul(pz[gi][:], gts[gi][:, frc, :], Y[:, frc, :],
                                                 start=(frc == 0), stop=(frc == 31)),
                         r=[f"gts{gi}", "Y"], w=[f"pz{gi}"], inc=(frc == 31))
                P.op("dve", lambda: dve.tensor_tensor(z_tok[:, tb, :], pz[gi][:], vx_tok[:, 1, tb, c0:c0 + 512], op=ALU.mult),
                     r=[f"pz{gi}", "vx_tok"], w=["z_tok"], nowaw=True)
            dft(lambda sc_: z_tok[:, sc_, :], "z_tok", 1, c0)
            for tq in range(16):
                gi = tq % 2
                P.dma("sp", gt2[gi][:], GT[tq], w=[f"gt2_{gi}"])
                P.dma("sp", x2t[gi][:], x2T[g * 4:(g + 1) * 4, :, tq * 128:(tq + 1) * 128].rearrange("c p t -> p c t"),
                      r=["x2T"], w=[f"x2t{gi}"])
                P.dma("sp", zht[gi][:], zhyT[g * 4:(g + 1) * 4, :, tq * 128:(tq + 1) * 128].rearrange("c p t -> p c t"),
                      r=["zhyT"], w=[f"zht{gi}"])
                for cc in range(4):
                    pi = rot(2)
                    for frc in range(32):
                        P.op("pe", lambda: pe.matmul(py[pi][:], Y[:, frc, cc * 128:(cc + 1) * 128], gt2[gi][:, frc, :],
                                                     start=(frc == 0), stop=(frc == 31)),
                             r=[f"gt2_{gi}", "Y"], w=[f"py{pi}"], inc=(frc == 31))
                    P.op("dve", lambda: dve.tensor_tensor(ytmp[pi][:], py[pi][:], x2t[gi][:, cc, :], op=ALU.mult),
                         r=[f"py{pi}", f"x2t{gi}"], w=[f"ytmp{pi}"])
                    P.op("pool", lambda: pool.tensor_tensor(yt[gi][:, cc, :], ytmp[pi][:], zht[gi][:, cc, :], op=ALU.mult),
                         r=[f"ytmp{pi}", f"zht{gi}"], w=[f"yt{gi}"], nowaw=True)
                P.store("sp", yhyT[g * 4:(g + 1) * 4, :, tq * 128:(tq + 1) * 128].rearrange("c p t -> p c t"), yt[gi][:],
                      r=[f"yt{gi}"], w=["yhyT"], sem=f"yt{gi}")
        P.phase_end()

    def phase_attn(l):
        v_at = hold["v_at"]
        P.phase_begin()
        cosT = P.sb("cosT", [128, L], F32)
        sinT = P.sb("sinT", [128, L], F32)
        RT = P.sb("RT", [128, 128], BF16)
        masks = P.sb("masks", [128, 2, 512], BF16)
        P.dma("sp", cosT[:], ropeC, w=["cosT"])
        P.dma("sp", sinT[:], ropeS, w=["sinT"])
        P.dma("sp", RT[:], RTd, w=["RT"])
        P.dma("sp", masks[:], masksd, w=["masks"])
        raw = [P.sb(f"raw{i}", [128, L], BF16) for i in range(2)]
        kr = P.sb("kr", [128, L], BF16)
        qr = P.sb("qr", [128, 4, L], BF16)
        zat = P.sb("zat", [128, 4, L], BF16)
        yat = P.sb("yat", [128, 4, L], BF16)
        esink = P.sb("esink", [128, 512], F32)
        rt1 = [P.sb(f"rt1_{i}", [128, 512], F32) for i in range(2)]
        rt2 = [P.sb(f"rt2_{i}", [128, 512], F32) for i in range(2)]
        PT = [P.sb(f"PT{i}", [128, 512], BF16) for i in range(6)]
        den = [P.sb(f"den{i}", [128, 512], F32) for i in range(2)]
        otmp = [P.sb(f"otmp{i}", [128, 512], F32) for i in range(2)]
        p_s = [P.ps(f"p_s{i}", [128, 512], F32) for i in range(6)]
        p_o = P.ps("p_o", [128, 512], F32)
        p_d = P.ps("p_d", [128, 512], F32)
        p_r = p_s[0]
        nraw = {"n": 0}
        scale = 1.0 / np.sqrt(128.0)

        def rope(src_dram, dst_fn, dstk):
            ri = nraw["n"] % 2
            nraw["n"] += 1
            P.dma("sp", raw[ri][:], src_dram, w=[f"raw{ri}"])
            for tq in range(4):
                sl = slice(tq * 512, (tq + 1) * 512)
                ti = rot(2)
                P.op("pe", lambda: pe.matmul(p_r[:], RT[:], raw[ri][:, sl], start=True, stop=True),
                     r=["RT", f"raw{ri}"], w=["p_s0"])
                P.op("pool", lambda: pool.tensor_tensor(rt1[ti][:], raw[ri][:, sl], cosT[:, sl], op=ALU.mult),
                     r=[f"raw{ri}", "cosT"], w=[f"rt1_{ti}"])
                P.op("dve", lambda: dve.tensor_tensor(rt2[ti][:], p_r[:], sinT[:, sl], op=ALU.mult),
                     r=["p_s0", "sinT"], w=[f"rt2_{ti}"])
                P.op("pool", lambda: pool.tensor_tensor(dst_fn(sl), rt1[ti][:], rt2[ti][:], op=ALU.add),
                     r=[f"rt1_{ti}", f"rt2_{ti}"], w=[dstk], nowaw=True)

        npt = {"n": 0}
        for g in range(2):
            rope(kT[g], lambda sl: kr[:, sl], "kr")
            for h in range(4):
                rope(qT[g * 4 + h], lambda sl: qr[:, h, sl], "qr")
            P.dma("sp", esink[:], sinkx[l, g:g + 1, :].partition_broadcast(128), w=["esink"])
            P.op("act", lambda: act.activation(esink[:], esink[:], AF.Exp), r=["esink"], w=["esink"])
            P.dma("sp", zat[:], zatT[g * 4:(g + 1) * 4].rearrange("c p t -> p c t"), r=["zatT"], w=["zat"])
            def stage1(qb):
                kbs = [kb for kb in (qb - 1, qb, qb + 1) if 0 <= kb < 16]
                qsl = slice(qb * 128, (qb + 1) * 128)
                pts = []
                for i, kb in enumerate(kbs):
                    pti = npt["n"] % 6
                    npt["n"] += 1
                    pts.append(pti)
                    P.op("pe", lambda: pe.matmul(p_s[pti][:].rearrange("p (h q) -> p h q", h=4), kr[:, kb * 128:(kb + 1) * 128],
                                                 qr[:, :, qsl], start=True, stop=True),
                         r=["kr", "qr"], w=[f"p_s{pti}"])
                    P.op("act", lambda: act.activation(PT[pti][:], p_s[pti][:], AF.Exp, scale=float(scale)),
                         r=[f"p_s{pti}"], w=[f"PT{pti}"])
                    if kb != qb:
                        mi = 0 if kb < qb else 1
                        P.op("dve", lambda: dve.tensor_tensor(PT[pti][:], PT[pti][:], masks[:, mi, :], op=ALU.mult),
                             r=[f"PT{pti}", "masks"], w=[f"PT{pti}"])
                return pts

            def stage2(qb, pts):
                qsl = slice(qb * 128, (qb + 1) * 128)
                n = len(pts)
                for i in range(n):
                    P.op("pe", lambda: pe.matmul(p_o[:], v_at[:, [kb for kb in (qb - 1, qb, qb + 1) if 0 <= kb < 16][i], g * 128:(g + 1) * 128],
                                                 PT[pts[i]][:], start=(i == 0), stop=(i == n - 1)),
                         r=["v_at", f"PT{pts[i]}"], w=["p_o"], inc=(i == n - 1))
                for i in range(n):
                    P.op("pe", lambda: pe.matmul(p_d[:], ones[:], PT[pts[i]][:], start=(i == 0), stop=(i == n - 1)),
                         r=["ones", f"PT{pts[i]}"], w=["p_d"], inc=(i == n - 1))
                oi = qb % 2
                P.op("dve", lambda: dve.tensor_tensor(den[oi][:], p_d[:], esink[:], op=ALU.add),
                     r=["p_d", "esink"], w=[f"den{oi}"])
                P.op("dve", lambda: dve.reciprocal(den[oi][:], den[oi][:]), r=[f"den{oi}"], w=[f"den{oi}"])
                P.op("dve", lambda: dve.tensor_tensor(otmp[oi][:], p_o[:], den[oi][:], op=ALU.mult),
                     r=["p_o", f"den{oi}"], w=[f"otmp{oi}"])
                P.op("pool", lambda: pool.tensor_tensor(yat[:, :, qsl], otmp[oi][:].rearrange("p (h q) -> p h q", h=4),
                                                        zat[:, :, qsl], op=ALU.mult),
                     r=[f"otmp{oi}", "zat"], w=["yat"], nowaw=True)

            nxt = stage1(0)
            for qb in range(16):
                cur = nxt
                if qb + 1 < 16:
                    nxt = stage1(qb + 1)
                stage2(qb, cur)
            P.store("sp", yatT[g * 4:(g + 1) * 4].rearrange("c p t -> p c t"), yat[:], r=["yat"], w=["yatT"], sem="yat")
        P.phase_end()

    def phase_merge(l):
        P.phase_begin()
        whs = P.sb("whs", [128, 8, D], BF16)
        was = P.sb("was", [128, 8, D], BF16)
        yh = P.sb("yh", [128, 8, L], BF16)
        ya = P.sb("ya", [128, 8, L], BF16)
        for hf in range(2):
            sl = slice(hf * 4, (hf + 1) * 4)
            P.dma("pool", whs[:, sl, :], w_hy[l].rearrange("(c p) n -> p c n", p=128)[:, sl, :], w=["whs"], sem=f"whs{hf}")
            P.dma("pool", was[:, sl, :], w_at[l].rearrange("(c p) n -> p c n", p=128)[:, sl, :], w=["was"], sem=f"was{hf}")
        P.dma("sp", yh[:], yhyT.rearrange("c p t -> p c t"), r=["yhyT"], w=["yh"])
        P.dma("sp", ya[:], yatT.rearrange("c p t -> p c t"), r=["yatT"], w=["ya"])
        sgh = [P.sb(f"sgh{i}", [128, L], BF16) for i in range(2)]
        sga = [P.sb(f"sga{i}", [128, L], BF16) for i in range(2)]
        mrow = [P.sb(f"mrow{i}", [128, L], BF16) for i in range(2)]
        m1 = [P.sb(f"m1_{i}", [128, 512], F32) for i in range(2)]
        m2 = [P.sb(f"m2_{i}", [128, 512], F32) for i in range(2)]
        pa = [P.ps(f"pa{i}", [128, 512], F32) for i in range(2)]
        pb = [P.ps(f"pb{i}", [128, 512], F32) for i in range(2)]
        for Dc in range(16):
            si = Dc % 2
            P.dma("sp", sgh[si][:], sgT[Dc], r=["sgT"], w=[f"sgh{si}"])
            P.dma("sp", sga[si][:], sgT[16 + Dc], r=["sgT"], w=[f"sga{si}"])
            for tq in range(4):
                tsl = slice(tq * 512, (tq + 1) * 512)
                pi = rot(2)
                for cc in range(8):
                    P.op("pe", lambda: pe.matmul(pa[pi][:], whs[:, cc, Dc * 128:(Dc + 1) * 128], yh[:, cc, tsl],
                                                 start=(cc == 0), stop=(cc == 7)),
                         r=["whs", "yh"], w=[f"pa{pi}"], inc=(cc == 7))
                for cc in range(8):
                    P.op("pe", lambda: pe.matmul(pb[pi][:], was[:, cc, Dc * 128:(Dc + 1) * 128], ya[:, cc, tsl],
                                                 start=(cc == 0), stop=(cc == 7)),
                         r=["was", "ya"], w=[f"pb{pi}"], inc=(cc == 7))
                P.op("dve", lambda: dve.tensor_tensor(m1[pi][:], pa[pi][:], sgh[si][:, tsl], op=ALU.mult),
                     r=[f"pa{pi}", f"sgh{si}"], w=[f"m1_{pi}"])
                P.op("dve", lambda: dve.tensor_tensor(m2[pi][:], pb[pi][:], sga[si][:, tsl], op=ALU.mult),
                     r=[f"pb{pi}", f"sga{si}"], w=[f"m2_{pi}"])
                P.op("pool", lambda: pool.tensor_tensor(mrow[si][:, tsl], m1[pi][:], m2[pi][:], op=ALU.add),
                     r=[f"m1_{pi}", f"m2_{pi}"], w=[f"mrow{si}"], nowaw=True)
            P.store("sp", mrgT[Dc], mrow[si][:], r=[f"mrow{si}"], w=["mrgT"], sem=f"mrow{si}")
        P.phase_end()

    def phase_out(l, xsrc, xdst, final):
        P.phase_begin()
        wo = P.sb("wo", [128, 16, D], BF16)
        wv = w_out[l].rearrange("(c p) n -> p c n", p=128)
        for qd in range(4):
            sl = slice(qd * 4, (qd + 1) * 4)
            P.dma("pool", wo[:, sl, :], wv[:, sl, :], w=["wo"], sem=f"wo{qd}")
        mt = [P.sb(f"mt{i}", [128, 16, 512], BF16) for i in range(2)]
        xt = [P.sb(f"oxt{i}", [128, D], F32) for i in range(2)]
        xo = [P.sb(f"oxo{i}", [128, D], F32) for i in range(2)]
        po = [P.ps(f"po{i}", [128, 512], F32) for i in range(4)]
        if final:
            g_bc = P.sb("gf_bc", [128, D], F32)
            P.dma("sp", g_bc[:], final_norm[0:1, :].partition_broadcast(128), w=["gf_bc"])
            sq = P.sb("osq", [128, D], F32)
            ss = [P.sb(f"oss{i}", [128, 1], F32) for i in range(2)]
            yo = [P.sb(f"oyo{i}", [128, D], F32) for i in range(2)]
        for t4 in range(4):
            mi = t4 % 2
            P.dma("sp", mt[mi][:], mrgT[:, :, t4 * 512:(t4 + 1) * 512].rearrange("c p t -> p c t"), r=["mrgT"], w=[f"mt{mi}"])
            for tt in range(4):
                tb = t4 * 4 + tt
                i = tb % 2
                P.dma("sp", xt[i][:], xsrc[tb * 128:(tb + 1) * 128, :], w=[f"oxt{i}"])
                for nq in range(4):
                    pi = rot(4)
                    nsl = slice(nq * 512, (nq + 1) * 512)
                    for Dc in range(16):
                        P.op("pe", lambda: pe.matmul(po[pi][:], mt[mi][:, Dc, tt * 128:(tt + 1) * 128], wo[:, Dc, nsl],
                                                     start=(Dc == 0), stop=(Dc == 15)),
                             r=[f"mt{mi}", "wo"], w=[f"po{pi}"], inc=(Dc == 15))
                    P.op("dve", lambda: dve.tensor_tensor(xo[i][:, nsl], po[pi][:], xt[i][:, nsl], op=ALU.add),
                         r=[f"po{pi}", f"oxt{i}"], w=[f"oxo{i}"], nowaw=True)
                if not final:
                    P.store("sp", xdst[tb * 128:(tb + 1) * 128, :], xo[i][:], r=[f"oxo{i}"], w=["xdst"], sem=f"oxo{i}")
                else:
                    P.op("act", lambda: act.activation(sq[:], xo[i][:], AF.Square, accum_out=ss[i][:]),
                         r=[f"oxo{i}"], w=["osq", f"oss{i}"])
                    P.op("dve", lambda: dve.tensor_scalar(ss[i][:], ss[i][:], 1.0 / D, 1e-6, op0=ALU.mult, op1=ALU.add),
                         r=[f"oss{i}"], w=[f"oss{i}"])
                    P.op("act", lambda: act.sqrt(ss[i][:], ss[i][:]), r=[f"oss{i}"], w=[f"oss{i}"])
                    P.op("dve", lambda: dve.reciprocal(ss[i][:], ss[i][:]), r=[f"oss{i}"], w=[f"oss{i}"])
                    P.op("dve", lambda: dve.scalar_tensor_tensor(yo[i][:], xo[i][:], ss[i][:, 0:1], g_bc[:],
                                                                 op0=ALU.mult, op1=ALU.mult),
                         r=[f"oxo{i}", f"oss{i}", "gf_bc"], w=[f"oyo{i}"])
                    P.store("sp", xdst[tb * 128:(tb + 1) * 128, :], yo[i][:], r=[f"oyo{i}"], w=["xdst"], sem=f"oyo{i}")
        P.phase_end()

    stop = [d for d in dbg if d.startswith("stop_")]
    stop = stop[0] if stop else None
    xsrc = x_in
    for l in range(nlayers):
        last = (l == nlayers - 1)
        phase_filter(l)
        if stop == "stop_filter":
            break
        P.phase_begin()
        hold["vx_tok"] = P.sb("vx_tok", [128, 2, 16, C], BF16)
        hold["v_at"] = P.sb("v_at", [128, 16, 256], BF16)
        phase_proj(l, xsrc)
        if stop == "stop_proj":
            break
        phase_hyena(l)
        if stop == "stop_hyena":
            break
        phase_attn(l)
        P.phase_end()
        if stop == "stop_attn":
            break
        phase_merge(l)
        if stop == "stop_merge":
            break
        phase_out(l, xsrc, out if last else xa, last)
        xsrc = xa
    P.finish()
    return nc, P


_SMALL = ("norm_g", "filt_w1", "filt_w2", "filt_w3", "filt_w4", "w_in", "w_hyena_out", "w_attn_out", "w_out")


def host_inputs(inputs):
    f32 = np.float32
    c = make_consts()
    shared = {}
    for k in _SMALL:
        shared[k] = np.ascontiguousarray(inputs[k], dtype=f32)
    shared["final_norm"] = np.ascontiguousarray(inputs["final_norm"], dtype=f32).reshape(1, D)
    cw = np.asarray(inputs["conv_w"], dtype=f32)
    shared["conv_wp"] = np.ascontiguousarray(cw.reshape(2, 3, 24, 128).transpose(0, 3, 1, 2))
    cb = np.asarray(inputs["conv_b"], dtype=f32)
    shared["conv_b"] = np.ascontiguousarray(cb)
    shared["conv_bp"] = np.ascontiguousarray(cb.reshape(2, 24, 128).transpose(0, 2, 1))
    shared["filt_sc"] = np.ascontiguousarray(np.stack(
        [np.asarray(inputs[k], dtype=f32) for k in ("filt_freq", "filt_b1", "filt_b2", "filt_b3")], axis=-1))
    shared["hyena_bias"] = np.ascontiguousarray(np.asarray(inputs["hyena_bias"], dtype=f32).reshape(2, 2 * C))
    sk = np.asarray(inputs["attn_sink"], dtype=f32)
    shared["sinkx"] = np.ascontiguousarray(np.repeat(sk.reshape(2, 2, 4, 1), 128, axis=3).reshape(2, 2, 512))
    for k in ("FT", "GT", "featsT", "decay", "ropeC", "ropeS", "RT", "masks"):
        shared[k] = c[k]
    x = np.asarray(inputs["x"], dtype=f32)
    return [dict(shared, x=np.ascontiguousarray(x[b])) for b in range(8)]


_NC = None


def kernel(**inputs):
    global _NC
    if _NC is None:
        _NC = build()[0]
    in_maps = host_inputs(inputs)
    res = run_bass_kernel_spmd(_NC, in_maps, core_ids=list(range(8)))
    return np.stack([np.asarray(r["out"], dtype=np.float32) for r in res.results], axis=0)
```

```python
import contextlib
import numpy as np
import ml_dtypes
import concourse.bass as bass
import concourse.mybir as mybir
from concourse.bass_utils import run_bass_kernel_spmd

F32 = mybir.dt.float32
BF16 = mybir.dt.bfloat16
AF = mybir.ActivationFunctionType
ALU = mybir.AluOpType
AX = mybir.AxisListType


class Prog:
    CE = ("pe", "act", "dve", "pool")

    def __init__(self, nc):
        self.nc = nc
        self.eng = {"pe": nc.tensor, "act": nc.scalar, "dve": nc.vector,
                    "pool": nc.gpsimd, "sp": nc.sync}
        self.csem = {e: nc.alloc_semaphore("s_" + e) for e in self.CE}
        self.cnt = {e: 0 for e in self.CE}
        self.pending = {e: False for e in self.CE}
        self.seen = {}
        self.kw = {}
        self.kr = {}
        self.dsem = {}
        self.dlast = {}
        self.stack = contextlib.ExitStack()
        self.phase_stack = None
        self.nops = 0

    def sb(self, name, shape, dtype, perm=False):
        st = self.stack if (perm or self.phase_stack is None) else self.phase_stack
        self.uid = getattr(self, "uid", 0) + 1
        return st.enter_context(self.nc.sbuf_tensor(f"sb{self.uid}_{name}", list(shape), dtype))

    def ps(self, name, shape, dtype=F32, perm=False):
        st = self.stack if (perm or self.phase_stack is None) else self.phase_stack
        self.uid = getattr(self, "uid", 0) + 1
        return st.enter_context(self.nc.psum_tensor(f"ps{self.uid}_{name}", list(shape), dtype))

    def phase_begin(self):
        self.scopes = getattr(self, "scopes", [])
        self.scopes.append(self.phase_stack)
        self.phase_stack = contextlib.ExitStack()

    def phase_end(self):
        self.barrier()
        self.phase_stack.close()
        self.phase_stack = self.scopes.pop()

    def _wait(self, F, tok):
        semname, sem, val, E = tok
        if E == "pe" and F == "pe":
            return
        k = (F, semname)
        if self.seen.get(k, 0) >= val:
            return
        self.eng[F].wait_ge(sem, val)
        self.seen[k] = val

    def _deps(self, F, r, w, nowaw):
        for k in r:
            for tok in self.kw.get(k, {}).values():
                self._wait(F, tok)
        for k in w:
            for tok in self.kr.get(k, {}).values():
                if tok[3] == F and F is not None and F in self.CE:
                    continue
                self._wait(F, tok)
            if not nowaw:
                for tok in self.kw.get(k, {}).values():
                    if tok[3] == F and F in self.CE:
                        continue
                    self._wait(F, tok)

    def _record(self, tok, r, w, nowaw):
        semname = tok[0]
        for k in w:
            if nowaw:
                self.kw.setdefault(k, {})[semname] = tok
            else:
                self.kw[k] = {semname: tok}
            self.kr[k] = {}
        for k in r:
            self.kr.setdefault(k, {})[semname] = tok

    def op(self, F, fn, r=(), w=(), nowaw=False, inc=True):
        self._flush(r, w)
        self._deps(F, r, w, nowaw)
        ins = fn()
        sem = self.csem[F]
        if inc:
            self.cnt[F] += 1
            ins.then_inc(sem, 1)
            self.pending[F] = False
            val = self.cnt[F]
        else:
            self.pending[F] = True
            val = self.cnt[F] + 1
        self._record(("s_" + F, sem, val, F), r, w, nowaw)
        self.nops += 1
        return ins

    def _flush(self, r=(), w=(), force=False):
        pend = getattr(self, "pend", None)
        if not pend:
            return
        if not force:
            rs, ws = set(r), set(w)
            hit = False
            for (_, _, _, pr, pw, _) in pend:
                if ws & (set(pr) | set(pw)) or rs & set(pw):
                    hit = True
                    break
            if not hit and len(pend) < 6:
                return
        self.pend = []
        for (Q, out, in_, pr, pw, sem) in pend:
            self.dma(Q, out, in_, r=pr, w=pw, sem=sem, _noflush=True)

    def store(self, Q, out, in_, r=(), w=(), sem=None):
        self._flush(r, w)
        self.pend = getattr(self, "pend", [])
        self.pend.append((Q, out, in_, tuple(r), tuple(w), sem))

    def dma(self, Q, out, in_, r=(), w=(), sem=None, nowaw=True, _noflush=False, **kw):
        if not _noflush:
            self._flush(r, w)
        semname = "d_" + (sem if sem is not None else (w[0] if w else r[0]))
        if semname not in self.dsem:
            self.dsem[semname] = [self.nc.alloc_semaphore(semname), 0]
        ent = self.dsem[semname]
        if semname in self.dlast:
            s, v = self.dlast[semname]
            self._wait(Q, (semname, s, v, None))
        self._deps(Q, r, w, nowaw)
        ins = self.eng[Q].dma_start(out=out, in_=in_, **kw)
        ent[1] += 16
        ins.then_inc(ent[0], 16)
        tok = (semname, ent[0], ent[1], None)
        self.dlast[semname] = (ent[0], ent[1])
        self._record(tok, r, w, nowaw)
        self.nops += 1
        return ins

    def barrier(self, engines=("pe", "act", "dve", "pool", "sp")):
        self._flush(force=True)
        for e in self.CE:
            assert not self.pending[e], e
        for F in engines:
            for E in self.CE:
                if self.cnt[E] > 0:
                    self._wait(F, ("s_" + E, self.csem[E], self.cnt[E], None))
            for semname, (s, v) in self.dlast.items():
                self._wait(F, (semname, s, v, None))

    def finish(self):
        self.barrier()
        while self.phase_stack is not None:
            self.phase_stack.close()
            self.phase_stack = self.scopes.pop() if getattr(self, "scopes", None) else None
        self.stack.close()


L = 2048
D = 2048
C = 1024
NW = 10752
MAGIC = 12582912.0
TWO_PI = 2.0 * np.pi

_CONSTS = None


def make_consts():
    global _CONSTS
    if _CONSTS is not None:
        return _CONSTS
    bf = ml_dtypes.bfloat16
    s = np.arange(L, dtype=np.float64)
    fidx = np.concatenate([np.arange(L // 2), (L - 1) - np.arange(L // 2)]).astype(np.float64)
    w = np.pi * (2 * fidx + 1) / (2 * L)
    ang = np.outer(s, w)
    Cm = np.cos(ang)
    Sm = np.sin(ang)
    M = np.empty((L, 32, 128))
    M[:, 0::2] = Cm.reshape(L, 16, 128)
    M[:, 1::2] = -Sm.reshape(L, 16, 128)
    FT = np.ascontiguousarray(M.reshape(16, 128, 32, 128).transpose(2, 1, 0, 3)).astype(bf)
    sperm = np.concatenate([np.arange(0, L, 2), np.arange(1, L, 2)])
    FTP = np.ascontiguousarray(M[sperm][:, :16].reshape(16, 128, 16, 128).transpose(2, 1, 0, 3)).astype(bf)
    G = np.empty((32, 128, L))
    G[0::2] = Cm.T.reshape(16, 128, L) * (2.0 / (2 * L))
    G[1::2] = -Sm.T.reshape(16, 128, L) * (2.0 / (2 * L))
    GT = np.ascontiguousarray(G.reshape(32, 128, 16, 128).transpose(2, 1, 0, 3)).astype(bf)
    del M, G, Cm, Sm, ang
    f32 = np.float32
    t = np.linspace(0.0, 1.0, L, dtype=f32)[:, None]
    bands = np.linspace(1e-4, 15.0, 16, dtype=f32)[None, :]
    a = (f32(2.0 * np.pi / L) * np.arange(L, dtype=f32)[:, None]) * bands
    feats = np.concatenate([t, np.cos(a), -np.sin(a)], axis=-1).astype(f32)
    featsT = np.ascontiguousarray(feats.T)
    mind = np.log(1e-2) / 0.3
    maxd = np.log(1e-2) / 1.5
    deltas = np.abs(np.linspace(mind, maxd, C, dtype=f32))
    dec = np.exp(-t * deltas[None, :]).astype(f32)
    dec_b = dec.copy()
    dec_b[0] = 0.0
    sperm2 = np.concatenate([np.arange(0, L, 2), np.arange(1, L, 2)])
    negt = np.ascontiguousarray((-t[:, 0])[sperm2].reshape(16, 128).T).astype(f32)
    deltas2 = np.ascontiguousarray(deltas.reshape(1, C)).astype(f32)
    inv = (500000.0 ** (-np.arange(0, 32, 2, dtype=f32) / f32(32))).astype(f32)
    ra = np.arange(L, dtype=f32)[None, :] * inv[:, None]
    ropeC = np.ones((128, L), f32)
    ropeS = np.zeros((128, L), f32)
    ropeC[0:16] = np.cos(ra)
    ropeC[16:32] = np.cos(ra)
    ropeS[0:16] = np.sin(ra)
    ropeS[16:32] = np.sin(ra)
    RT = np.zeros((128, 128), f32)
    for i in range(16):
        RT[i + 16, i] = -1.0
        RT[i, i + 16] = 1.0
    kl = np.arange(128)[:, None]
    ql = np.arange(128)[None, :]
    m0 = (kl >= ql).astype(f32)
    m1 = (kl <= ql).astype(f32)
    masks = np.stack([np.tile(m0, (1, 4)), np.tile(m1, (1, 4))], axis=1)
    _CONSTS = dict(FT=FT, FTP=FTP, GT=GT, featsT=featsT, deltas=deltas2, negt=negt, ropeC=ropeC, ropeS=ropeS,
                   RT=RT.astype(bf), masks=np.ascontiguousarray(masks).astype(bf))
    return _CONSTS


def build(nlayers=2, dbg=()):
    nc = bass.Bass("TRN2", target_bir_lowering=False)
    P = Prog(nc)
    act, dve, pool, pe = nc.scalar, nc.vector, nc.gpsimd, nc.tensor

    def din(name, shape, dt=F32):
        return nc.dram_tensor(name, list(shape), dt, kind="ExternalInput").ap()

    def dscr(name, shape, dt):
        kind = "ExternalOutput" if name in dbg else "Internal"
        return nc.dram_tensor(name, list(shape), dt, kind=kind).ap()

    x_in = din("x", [L, D])
    norm_g = din("norm_g", [2, D])
    final_norm = din("final_norm", [1, D])
    w_in = din("w_in", [2, D, NW])
    conv_wp = din("conv_wp", [2, 128, 3, 24])
    conv_b = din("conv_b", [2, 3 * C])
    conv_bp = din("conv_bp", [2, 128, 24])
    fw1 = din("filt_w1", [2, 33, 64])
    fw2 = din("filt_w2", [2, 64, 64])
    fw3 = din("filt_w3", [2, 64, 64])
    fw4 = din("filt_w4", [2, 64, 4 * C])
    fsc = din("filt_sc", [2, 64, 4])
    hbias = din("hyena_bias", [2, 2 * C])
    sinkx = din("sinkx", [2, 2, 512])
    w_hy = din("w_hyena_out", [2, C, D])
    w_at = din("w_attn_out", [2, C, D])
    w_out = din("w_out", [2, D, D])
    FT = din("FT", [32, 128, 16, 128], BF16)
    GT = din("GT", [16, 128, 32, 128], BF16)
    FTP = din("FTP", [16, 128, 16, 128], BF16)
    featsT = din("featsT", [33, L])
    deltas_d = din("deltas", [1, C])
    negt_d = din("negt", [128, 16])
    ropeC = din("ropeC", [128, L])
    ropeS = din("ropeS", [128, L])
    RTd = din("RT", [128, 128], BF16)
    masksd = din("masks", [128, 2, 512], BF16)

    out = nc.dram_tensor("out", [L, D], F32, kind="ExternalOutput").ap()
    Kf = dscr("Kf", [2, 32, 128, C], BF16)
    x2T = dscr("x2T", [8, 128, L], BF16)
    zhyT = dscr("zhyT", [8, 128, L], BF16)
    qT = dscr("qT", [8, 128, L], BF16)
    kT = dscr("kT", [2, 128, L], BF16)
    zatT = dscr("zatT", [8, 128, L], BF16)
    sgT = dscr("sgT", [32, 128, L], BF16)
    yhyT = dscr("yhyT", [8, 128, L], BF16)
    yatT = dscr("yatT", [8, 128, L], BF16)
    mrgT = dscr("mrgT", [16, 128, L], BF16)
    xa = dscr("xa", [L, D], F32)
    dbg_vx = dscr("dbg_vx", [128, 2, 16, C], BF16) if "dbg_vx" in dbg else None
    dbg_vat = dscr("dbg_vat", [128, 16, 256], BF16) if "dbg_vat" in dbg else None

    ident = P.sb("ident", [128, 128], BF16, perm=True)
    ones = P.sb("ones", [128, 128], BF16, perm=True)
    hold = {}
    P.op("pool", lambda: pool.memset(ident[:], 1.0), w=["ident"])
    P.op("pool", lambda: pool.affine_select(ident[:], ident[:], pattern=[[-1, 128]],
                                            compare_op=ALU.is_equal, fill=0.0, base=0,
                                            channel_multiplier=1), r=["ident"], w=["ident"])
    P.op("pool", lambda: pool.memset(ones[:], 1.0), w=["ones"])

    rr = {"n": 0}

    def rot(n):
        rr["n"] += 1
        return rr["n"] % n

    def sin_layer(lhsT_ap, rhs_fn, freq_ap, fb_ap, ag, md, hout, pss):
        for q in range(4):
            ps = pss[q % 2]
            psk = f"fps{q % 2}"
            P.op("pe", lambda: pe.matmul(ps[:], lhsT_ap, rhs_fn(q), start=True, stop=True),
                 r=["fw", "fh"], w=[psk])
            P.op("dve", lambda: dve.tensor_scalar(ag[:, q * 512:(q + 1) * 512], ps[:], freq_ap, fb_ap,
                                                  op0=ALU.mult, op1=ALU.add), r=[psk, "fsc"], w=["ag"], nowaw=True)
        P.op("dve", lambda: dve.tensor_scalar(md[:], ag[:], 1.0 / TWO_PI, MAGIC, op0=ALU.mult, op1=ALU.add),
             r=["ag"], w=["md"])
        P.op("dve", lambda: dve.tensor_scalar(md[:], md[:], MAGIC, None, op0=ALU.subtract), r=["md"], w=["md"])
        P.op("dve", lambda: dve.scalar_tensor_tensor(md[:], md[:], -TWO_PI, ag[:], op0=ALU.mult, op1=ALU.add),
             r=["md", "ag"], w=["md"])
        P.op("act", lambda: act.activation(hout[:], md[:], AF.Sin, scale=0.999999), r=["md"], w=["fh"])

    def phase_filter(l):
        P.phase_begin()
        fT = P.sb("featsT", [33, L], F32)
        w1 = P.sb("fw1", [33, 64], F32)
        w2 = P.sb("fw2", [64, 64], F32)
        w3 = P.sb("fw3", [64, 64], F32)
        w4 = P.sb("fw4", [64, 4 * C], F32)
        sc = P.sb("fsc", [64, 4], F32)
        fb = P.sb("ffb", [64, 3], F32)
        ag = P.sb("fag", [64, L], F32)
        md = P.sb("fmd", [64, L], F32)
        hA = P.sb("fhA", [64, L], F32)
        hB = P.sb("fhB", [64, L], F32)
        hb_bc = P.sb("hb_bc", [128, 2 * C], F32)
        ksum = P.sb("ksum", [128, 16, C], BF16)
        kdif = P.sb("kdif", [128, 16, C], BF16)
        pss = [P.ps(f"fps{i}", [128, 512], F32) for i in range(8)]
        P.dma("sp", fT[:], featsT, w=["fh"], sem="f_fT")
        P.dma("sp", w1[:], fw1[l], w=["fw"], sem="f_w1")
        P.dma("sp", w2[:], fw2[l], w=["fw"], sem="f_w2")
        P.dma("sp", w3[:], fw3[l], w=["fw"], sem="f_w3")
        P.dma("sp", w4[:], fw4[l], w=["fw"], sem="f_w4")
        P.dma("sp", sc[:], fsc[l], w=["fsc"], sem="f_sc")
        P.dma("sp", hb_bc[:], hbias[l:l + 1, :].partition_broadcast(128), w=["hb_bc"])
        P.op("dve", lambda: dve.tensor_scalar(fb[:], sc[:, 1:4], sc[:, 0:1], None, op0=ALU.mult),
             r=["fsc"], w=["fsc2"])
        P.kw["fsc"].update(P.kw["fsc2"])
        ps64 = [p_[0:64, :] for p_ in pss]
        sin_layer(w1[:], lambda q: fT[:, q * 512:(q + 1) * 512], sc[:, 0:1], fb[:, 0:1], ag, md, hA, ps64)
        sin_layer(w2[:], lambda q: hA[:, q * 512:(q + 1) * 512], sc[:, 0:1], fb[:, 1:2], ag, md, hB, ps64)
        sin_layer(w3[:], lambda q: hB[:, q * 512:(q + 1) * 512], sc[:, 0:1], fb[:, 2:3], ag, md, hA, ps64)
        dct = [P.sb(f"dct{i}", [128, C], F32) for i in range(2)]
        dcz = P.sb("dcz", [128, C], F32)
        dl_bc = P.sb("dl_bc", [128, C], F32)
        negt = P.sb("negt", [128, 16], F32)
        P.dma("sp", dl_bc[:], deltas_d[0:1, :].partition_broadcast(128), w=["dl_bc"])
        P.dma("sp", negt[:], negt_d, w=["negt"])
        tf = [P.sb(f"ftf{i}", [128, 512], F32) for i in range(2)]
        tb_ = [P.sb(f"ftb{i}", [128, 512], F32) for i in range(2)]
        fts4 = [P.sb(f"fts{i}", [128, 16, 128], BF16) for i in range(4)]
        kst = [P.sb(f"kst{i}", [128, C], BF16) for i in range(4)]
        tkb = [P.sb(f"tkb{i}", [128, 512], F32) for i in range(2)]
        nd = 0
        for o in range(2):
            for sc_ in range(16):
                par, blk = sc_ // 8, sc_ % 8
                t0 = par + 256 * blk
                di = nd % 2
                nd += 1
                P.op("act", lambda: act.activation(dct[di][:], dl_bc[:], AF.Exp, scale=negt[:, sc_:sc_ + 1]),
                     r=["dl_bc", "negt"], w=[f"dct{di}"])
                if sc_ == 0:
                    P.op("act", lambda: act.activation(dcz[:], dct[di][:], AF.Copy), r=[f"dct{di}"], w=["dcz"])
                    P.op("dve", lambda: dve.memset(dcz[0:1, :], 0.0), r=["dcz"], w=["dcz"])
                for half in range(2):
                    i = rot(2)
                    c0 = half * 512
                    for dr in range(2):
                        n0 = o * 2048 + dr * 1024 + c0
                        ps = pss[2 * i + dr]
                        P.op("pe", lambda: pe.matmul(ps[:], hA[:, t0:t0 + 255:2], w4[:, n0:n0 + 512],
                                                     start=True, stop=True), r=["fh", "fw"], w=[f"fps{2 * i + dr}"])
                        tt = (tf, tb_)[dr][i]
                        dsrc = dcz if (dr == 1 and sc_ == 0) else dct[di]
                        P.op("dve", lambda: dve.tensor_tensor(tt[:], ps[:], dsrc[:, c0:c0 + 512], op=ALU.mult),
                             r=[f"fps{2 * i + dr}", f"dct{di}", "dcz"], w=[f"ft{dr}{i}"])
                    P.op("pool", lambda: pool.tensor_tensor(ksum[:, sc_, c0:c0 + 512], tf[i][:], tb_[i][:],
                                                            op=ALU.add), r=[f"ft0{i}", f"ft1{i}"], w=["ksum"], nowaw=True)
                    P.op("dve", lambda: dve.tensor_tensor(kdif[:, sc_, c0:c0 + 512], tf[i][:], tb_[i][:],
                                                          op=ALU.subtract), r=[f"ft0{i}", f"ft1{i}"], w=["kdif"], nowaw=True)
            for fb in range(8):
                fo = 2 * (fb % 2)
                fts = fts4[fo:fo + 2]
                P.dma("sp", fts[0][:], FTP[2 * fb], w=[f"fts{fo}"])
                P.dma("sp", fts[1][:], FTP[2 * fb + 1], w=[f"fts{fo + 1}"])
                for nq in range(2):
                    csl = slice(nq * 512, (nq + 1) * 512)
                    pb = 4 * rot(2)
                    for pi, (fi, src, srck, lo) in enumerate(((0, ksum, "ksum", 0), (0, ksum, "ksum", 8),
                                                             (1, kdif, "kdif", 0), (1, kdif, "kdif", 8))):
                        for k_ in range(8):
                            P.op("pe", lambda: pe.matmul(pss[pb + pi][:], fts[fi][:, lo + k_, :], src[:, lo + k_, csl],
                                                         start=(k_ == 0), stop=(k_ == 7)),
                                 r=[f"fts{fo + fi}", srck], w=[f"fps{pb + pi}"], inc=(k_ == 7))
                    ti = rot(2)
                    P.op("dve", lambda: dve.tensor_tensor(tkb[ti][:], pss[pb][:], hb_bc[:, o * C + nq * 512:o * C + (nq + 1) * 512],
                                                          op=ALU.add), r=[f"fps{pb}", "hb_bc"], w=[f"tkb{ti}"])
                    P.op("dve", lambda: dve.tensor_tensor(kst[0][:, csl], tkb[ti][:], pss[pb + 1][:], op=ALU.add),
                         r=[f"tkb{ti}", f"fps{pb + 1}"], w=["kst0"], nowaw=True)
                    P.op("dve", lambda: dve.tensor_tensor(kst[2][:, csl], tkb[ti][:], pss[pb + 1][:], op=ALU.subtract),
                         r=[f"tkb{ti}", f"fps{pb + 1}"], w=["kst2"], nowaw=True)
                    ui = rot(2)
                    P.op("act", lambda: act.activation(tf[ui][:], pss[pb + 2][:], AF.Copy), r=[f"fps{pb + 2}"], w=[f"ft0{ui}"])
                    P.op("dve", lambda: dve.tensor_tensor(kst[1][:, csl], pss[pb + 3][:], tf[ui][:], op=ALU.add),
                         r=[f"fps{pb + 3}", f"ft0{ui}"], w=["kst1"], nowaw=True)
                    P.op("dve", lambda: dve.tensor_tensor(kst[3][:, csl], pss[pb + 3][:], tf[ui][:], op=ALU.subtract),
                         r=[f"fps{pb + 3}", f"ft0{ui}"], w=["kst3"], nowaw=True)
                for j, frb in enumerate((2 * fb, 2 * fb + 1, 2 * (fb + 8), 2 * (fb + 8) + 1)):
                    P.store("sp", Kf[o, frb], kst[j][:], r=[f"kst{j}"], w=["Kf"], sem=f"kst{j}")
        P.phase_end()

    def phase_proj(l, xsrc):
        vx_tok, v_at = hold["vx_tok"], hold["v_at"]
        P.phase_begin()
        hT = P.sb("hT", [128, 16, L], BF16)
        P.phase_begin()
        g_bc = P.sb("g_bc", [128, D], F32)
        P.dma("sp", g_bc[:], norm_g[l:l + 1, :].partition_broadcast(128), w=["g_bc"])
        xt = [P.sb(f"xt{i}", [128, D], F32) for i in range(2)]
        sq = P.sb("sq", [128, D], F32)
        ss = [P.sb(f"ss{i}", [128, 1], F32) for i in range(2)]
        xn = [P.sb(f"xn{i}", [128, D], BF16) for i in range(2)]
        pt = [P.ps(f"pt{i}", [128, 4, 128], BF16) for i in range(2)]
        for tb in range(16):
            i = tb % 2
            P.dma("sp", xt[i][:], xsrc[tb * 128:(tb + 1) * 128, :], w=[f"xt{i}"])
            P.op("act", lambda: act.activation(sq[:], xt[i][:], AF.Square, accum_out=ss[i][:]),
                 r=[f"xt{i}"], w=["sq", f"ss{i}"])
            P.op("dve", lambda: dve.tensor_scalar(ss[i][:], ss[i][:], 1.0 / D, 1e-6, op0=ALU.mult, op1=ALU.add),
                 r=[f"ss{i}"], w=[f"ss{i}"])
            P.op("act", lambda: act.sqrt(ss[i][:], ss[i][:]), r=[f"ss{i}"], w=[f"ss{i}"])
            P.op("dve", lambda: dve.reciprocal(ss[i][:], ss[i][:]), r=[f"ss{i}"], w=[f"ss{i}"])
            P.op("dve", lambda: dve.scalar_tensor_tensor(xn[i][:], xt[i][:], ss[i][:, 0:1], g_bc[:],
                                                         op0=ALU.mult, op1=ALU.mult),
                 r=[f"xt{i}", f"ss{i}", "g_bc"], w=[f"xn{i}"])
            for q in range(4):
                j = rot(2)
                for c in range(4):
                    dc = q * 4 + c
                    P.op("pe", lambda: pe.transpose(pt[j][:, c, :], xn[i][:, dc * 128:(dc + 1) * 128], ident[:]),
                         r=[f"xn{i}", "ident"], w=[f"pt{j}"], inc=(c == 3))
                P.op("act", lambda: act.activation(hT[:, q * 4:(q + 1) * 4, tb * 128:(tb + 1) * 128], pt[j][:], AF.Copy),
                     r=[f"pt{j}"], w=["hT"], nowaw=True)

        P.phase_end()
        wsl = [P.sb(f"wsl{i}", [128, 16, 256], BF16) for i in range(3)]
        upad = [P.sb(f"upad{i}", [128, L + 2], BF16) for i in range(2)]
        stg = [P.sb(f"stg{i}", [128, L], BF16) for i in range(3)]
        dg = [[P.sb(f"dg{i}_{j}", [128, 128], BF16) for j in range(3)] for i in range(2)]
        cw = P.sb("cw", [128, 3, 24], F32)
        cbp = P.sb("cbp", [128, 24], F32)
        cbrow = P.sb("cbrow", [1, 3 * C], BF16)
        P.dma("sp", cw[:], conv_wp[l], w=["cw"])
        P.dma("sp", cbp[:], conv_bp[l], w=["cbp"])
        P.dma("pool", cbrow[:], conv_b[l:l + 1, :], w=["cbrow"])
        for i in range(2):
            P.op("pool", lambda: pool.memset(upad[i][:, 0:1], 0.0), w=[f"upad{i}"], nowaw=True)
            P.op("pool", lambda: pool.memset(upad[i][:, L + 1:L + 2], 0.0), w=[f"upad{i}"], nowaw=True)
        pm = [P.ps(f"pm{i}", [128, 512], F32) for i in range(4)]
        pc = [P.ps(f"pc{i}", [128, 4, 128], F32) for i in range(2)]
        w_l = w_in[l].rearrange("(c p) n -> p c n", p=128)

        def load_slab(si):
            b = si % 3
            P.dma("pool", wsl[b][:], w_l[:, :, si * 256:(si + 1) * 256], w=[f"wsl{b}"])

        def fm_matmul(b, j, tq):
            pi = rot(4)
            for dc in range(16):
                P.op("pe", lambda: pe.matmul(pm[pi][:], wsl[b][:, dc, j * 128:(j + 1) * 128],
                                             hT[:, dc, tq * 512:(tq + 1) * 512],
                                             start=(dc == 0), stop=(dc == 15)),
                     r=[f"wsl{b}", "hT"], w=[f"pm{pi}"], inc=(dc == 15))
            return pi

        load_slab(0)
        load_slab(1)
        stg_n = 0
        for si in range(42):
            if si + 2 < 42:
                load_slab(si + 2)
            b = si % 3
            for j in range(2):
                n = 2 * si + j
                if n < 24:
                    ui = n % 2
                    for tq in range(4):
                        pi = fm_matmul(b, j, tq)
                        P.op("act", lambda: act.activation(upad[ui][:, 1 + tq * 512:1 + (tq + 1) * 512], pm[pi][:], AF.Copy),
                             r=[f"pm{pi}"], w=[f"upad{ui}"], nowaw=True)
                    for tap in range(3):
                        P.op("dve", lambda: dve.tensor_scalar(dg[ui][tap][:], ident[:], cw[:, tap, n:n + 1], None,
                                                              op0=ALU.mult), r=["ident", "cw"], w=[f"dg{ui}"], nowaw=True)
                    stream, cc = n // 8, n % 8
                    if stream < 2:
                        for t4 in range(4):
                            ci = rot(2)
                            for tt in range(4):
                                tb = t4 * 4 + tt
                                for tap in range(3):
                                    P.op("pe", lambda: pe.matmul(pc[ci][:, tt, :], upad[ui][:, tb * 128 + tap:tb * 128 + tap + 128],
                                                                 dg[ui][tap][:], start=(tap == 0), stop=False),
                                         r=[f"upad{ui}", f"dg{ui}"], w=[f"pc{ci}"], inc=False)
                                P.op("pe", lambda: pe.matmul(pc[ci][:, tt, :], ones[0:1, :], cbrow[0:1, n * 128:(n + 1) * 128],
                                                             start=False, stop=True),
                                     r=["ones", "cbrow"], w=[f"pc{ci}"], inc=(tt == 3))
                            P.op("act", lambda: act.activation(vx_tok[:, stream, t4 * 4:(t4 + 1) * 4, cc * 128:(cc + 1) * 128],
                                                               pc[ci][:], AF.Copy),
                                 r=[f"pc{ci}"], w=["vx_tok"], nowaw=True)
                    else:
                        sgi = stg_n % 3
                        stg_n += 1
                        for tq in range(4):
                            pi = rot(4)
                            for tap in range(3):
                                P.op("pe", lambda: pe.matmul(pm[pi][:], dg[ui][tap][:],
                                                             upad[ui][:, tq * 512 + tap:tq * 512 + tap + 512],
                                                             start=(tap == 0), stop=(tap == 2)),
                                     r=[f"upad{ui}", f"dg{ui}"], w=[f"pm{pi}"], inc=(tap == 2))
                            P.op("act", lambda: act.activation(stg[sgi][:, tq * 512:(tq + 1) * 512], pm[pi][:], AF.Identity,
                                                               bias=cbp[:, n:n + 1]),
                                 r=[f"pm{pi}", "cbp"], w=[f"stg{sgi}"], nowaw=True)
                        P.store("sp", x2T[cc], stg[sgi][:], r=[f"stg{sgi}"], w=["x2T"], sem=f"stg{sgi}")
                elif n in (42, 43):
                    for t4 in range(4):
                        ci = rot(2)
                        for tt in range(4):
                            tb = t4 * 4 + tt
                            for dc in range(16):
                                P.op("pe", lambda: pe.matmul(pc[ci][:, tt, :], hT[:, dc, tb * 128:(tb + 1) * 128],
                                                             wsl[b][:, dc, j * 128:(j + 1) * 128],
                                                             start=(dc == 0), stop=(dc == 15)),
                                     r=[f"wsl{b}", "hT"], w=[f"pc{ci}"], inc=(dc == 15 and tt == 3))
                        P.op("act", lambda: act.activation(v_at[:, t4 * 4:(t4 + 1) * 4, (n - 42) * 128:(n - 41) * 128],
                                                           pc[ci][:], AF.Copy),
                             r=[f"pc{ci}"], w=["v_at"], nowaw=True)
                else:
                    if n < 32:
                        fn, dst, dk = AF.Silu, zhyT[n - 24], "zhyT"
                    elif n < 40:
                        fn, dst, dk = AF.Copy, qT[n - 32], "qT"
                    elif n < 42:
                        fn, dst, dk = AF.Copy, kT[n - 40], "kT"
                    elif n < 52:
                        fn, dst, dk = AF.Silu, zatT[n - 44], "zatT"
                    else:
                        fn, dst, dk = AF.Sigmoid, sgT[n - 52], "sgT"
                    sgi = stg_n % 3
                    stg_n += 1
                    for tq in range(4):
                        pi = fm_matmul(b, j, tq)
                        P.op("act", lambda: act.activation(stg[sgi][:, tq * 512:(tq + 1) * 512], pm[pi][:], fn),
                             r=[f"pm{pi}"], w=[f"stg{sgi}"], nowaw=True)
                    P.store("sp", dst, stg[sgi][:], r=[f"stg{sgi}"], w=[dk], sem=f"stg{sgi}")
        if dbg_vx is not None:
            P.dma("sp", dbg_vx, vx_tok[:], r=["vx_tok"], w=["dbg_vx"], sem="dbgvx")
        if dbg_vat is not None:
            P.dma("sp", dbg_vat, v_at[:], r=["v_at"], w=["dbg_vat"], sem="dbgvat")
        P.phase_end()

    def phase_hyena(l):
        vx_tok = hold["vx_tok"]
        P.phase_begin()
        Y = P.sb("Y", [128, 32, 512], BF16)
        z_tok = P.sb("z_tok", [128, 16, 512], BF16)
        fts = [P.sb(f"hfts{i}", [128, 16, 128], BF16) for i in range(3)]
        gts = [P.sb(f"gts{i}", [128, 32, 128], BF16) for i in range(2)]
        gt2 = [P.sb(f"gt2_{i}", [128, 32, 128], BF16) for i in range(2)]
        kri = [P.sb(f"kri{i}", [128, 2, 512], BF16) for i in range(2)]
        tmp = [[P.sb(f"htmp{i}_{j}", [128, 512], F32) for j in range(4)] for i in range(2)]
        x2t = [P.sb(f"x2t{i}", [128, 4, 128], BF16) for i in range(2)]
        zht = [P.sb(f"zht{i}", [128, 4, 128], BF16) for i in range(2)]
        yt = [P.sb(f"yt{i}", [128, 4, 128], BF16) for i in range(2)]
        ytmp = [P.sb(f"ytmp{i}", [128, 128], F32) for i in range(2)]
        pu = [P.ps(f"pu{i}", [128, 512], F32) for i in range(4)]
        pz = [P.ps(f"pz{i}", [128, 512], F32) for i in range(2)]
        py = [P.ps(f"py{i}", [128, 128], F32) for i in range(2)]
        nft = {"n": 0}

        def dft(src_fn, srck, o, c0):
            for fb in range(16):
                ki = fb % 2
                P.dma("sp", kri[ki][:], Kf[o, 2 * fb:2 * fb + 2, :, c0:c0 + 512].rearrange("k p c -> p k c"),
                      r=["Kf"], w=[f"kri{ki}"])
                pis = []
                for kind in range(2):
                    frb = 2 * fb + kind
                    fi = nft["n"] % 3
                    nft["n"] += 1
                    P.dma("sp", fts[fi][:], FT[frb], w=[f"hfts{fi}"])
                    pi = 2 * (fb % 2) + kind
                    pis.append(pi)
                    for sc_ in range(16):
                        P.op("pe", lambda: pe.matmul(pu[pi][:], fts[fi][:, sc_, :], src_fn(sc_),
                                                     start=(sc_ == 0), stop=(sc_ == 15)),
                             r=[f"hfts{fi}", srck], w=[f"pu{pi}"], inc=(sc_ == 15))
                ur, ui = pu[pis[0]], pu[pis[1]]
                urk, uik = f"pu{pis[0]}", f"pu{pis[1]}"
                t = tmp[ki]
                tk = f"htmp{ki}"
                kr_, ki_ = kri[ki][:, 0, :], kri[ki][:, 1, :]
                P.op("dve", lambda: dve.tensor_tensor(t[0][:], ur[:], kr_, op=ALU.mult), r=[urk, f"kri{ki}"], w=[tk + "a"])
                P.op("dve", lambda: dve.tensor_tensor(t[1][:], ui[:], ki_, op=ALU.mult), r=[uik, f"kri{ki}"], w=[tk + "b"])
                P.op("dve", lambda: dve.tensor_tensor(t[2][:], ur[:], ki_, op=ALU.mult), r=[urk, f"kri{ki}"], w=[tk + "c"])
                P.op("dve", lambda: dve.tensor_tensor(t[3][:], ui[:], kr_, op=ALU.mult), r=[uik, f"kri{ki}"], w=[tk + "d"])
                P.op("pool", lambda: pool.tensor_tensor(Y[:, 2 * fb, :], t[0][:], t[1][:], op=ALU.subtract),
                     r=[tk + "a", tk + "b"], w=["Y"], nowaw=True)
                P.op("pool", lambda: pool.tensor_tensor(Y[:, 2 * fb + 1, :], t[2][:], t[3][:], op=ALU.add),
                     r=[tk + "c", tk + "d"], w=["Y"], nowaw=True)

        for g in range(2):
            c0 = g * 512
            dft(lambda sc_: vx_tok[:, 0, sc_, c0:c0 + 512], "vx_tok", 0, c0)
            for tb in range(16):
                gi = tb % 2
                P.dma("sp", gts[gi][:], GT[tb], w=[f"gts{gi}"])
                for frc in range(32):
                    P.op("pe", lambda: pe.matmul(pz[gi][:], gts[gi][:, frc, :], Y[:, frc, :],
                                                 start=(frc == 0), stop=(frc == 31)),
                         r=[f"gts{gi}", "Y"], w=[f"pz{gi}"], inc=(frc == 31))
                P.op("dve", lambda: dve.tensor_tensor(z_tok[:, tb, :], pz[gi][:], vx_tok[:, 1, tb, c0:c0 + 512], op=ALU.mult),
                     r=[f"pz{gi}", "vx_tok"], w=["z_tok"], nowaw=True)
            dft(lambda sc_: z_tok[:, sc_, :], "z_tok", 1, c0)
            for tq in range(16):
                gi = tq % 2
                P.dma("sp", gt2[gi][:], GT[tq], w=[f"gt2_{gi}"])
                P.dma("sp", x2t[gi][:], x2T[g * 4:(g + 1) * 4, :, tq * 128:(tq + 1) * 128].rearrange("c p t -> p c t"),
                      r=["x2T"], w=[f"x2t{gi}"])
                P.dma("sp", zht[gi][:], zhyT[g * 4:(g + 1) * 4, :, tq * 128:(tq + 1) * 128].rearrange("c p t -> p c t"),
                      r=["zhyT"], w=[f"zht{gi}"])
                for cc in range(4):
                    pi = rot(2)
                    for frc in range(32):
                        P.op("pe", lambda: pe.matmul(py[pi][:], Y[:, frc, cc * 128:(cc + 1) * 128], gt2[gi][:, frc, :],
                                                     start=(frc == 0), stop=(frc == 31)),
                             r=[f"gt2_{gi}", "Y"], w=[f"py{pi}"], inc=(frc == 31))
                    P.op("dve", lambda: dve.tensor_tensor(ytmp[pi][:], py[pi][:], x2t[gi][:, cc, :], op=ALU.mult),
                         r=[f"py{pi}", f"x2t{gi}"], w=[f"ytmp{pi}"])
                    P.op("pool", lambda: pool.tensor_tensor(yt[gi][:, cc, :], ytmp[pi][:], zht[gi][:, cc, :], op=ALU.mult),
                         r=[f"ytmp{pi}", f"zht{gi}"], w=[f"yt{gi}"], nowaw=True)
                P.store("sp", yhyT[g * 4:(g + 1) * 4, :, tq * 128:(tq + 1) * 128].rearrange("c p t -> p c t"), yt[gi][:],
                      r=[f"yt{gi}"], w=["yhyT"], sem=f"yt{gi}")
        P.phase_end()

    def phase_attn(l):
        v_at = hold["v_at"]
        P.phase_begin()
        cosT = P.sb("cosT", [128, L], F32)
        sinT = P.sb("sinT", [128, L], F32)
        RT = P.sb("RT", [128, 128], BF16)
        masks = P.sb("masks", [128, 2, 512], BF16)
        P.dma("sp", cosT[:], ropeC, w=["cosT"])
        P.dma("sp", sinT[:], ropeS, w=["sinT"])
        P.dma("sp", RT[:], RTd, w=["RT"])
        P.dma("sp", masks[:], masksd, w=["masks"])
        raw = [P.sb(f"raw{i}", [128, L], BF16) for i in range(2)]
        kr = P.sb("kr", [128, L], BF16)
        qr = P.sb("qr", [128, 4, L], BF16)
        zat = P.sb("zat", [128, 4, L], BF16)
        yat = P.sb("yat", [128, 4, L], BF16)
        esink = P.sb("esink", [128, 512], F32)
        rt1 = [P.sb(f"rt1_{i}", [128, 512], F32) for i in range(2)]
        rt2 = [P.sb(f"rt2_{i}", [128, 512], F32) for i in range(2)]
        PT = [P.sb(f"PT{i}", [128, 512], BF16) for i in range(6)]
        den = [P.sb(f"den{i}", [128, 512], F32) for i in range(2)]
        otmp = [P.sb(f"otmp{i}", [128, 512], F32) for i in range(2)]
        p_s = [P.ps(f"p_s{i}", [128, 512], F32) for i in range(4)]
        p_o2 = [P.ps(f"p_o{i}", [128, 512], F32) for i in range(2)]
        p_d2 = [P.ps(f"p_d{i}", [128, 512], F32) for i in range(2)]
        p_r = p_s[0]
        nraw = {"n": 0}
        scale = 1.0 / np.sqrt(128.0)

        def rope(src_dram, dst_fn, dstk):
            ri = nraw["n"] % 2
            nraw["n"] += 1
            P.dma("sp", raw[ri][:], src_dram, w=[f"raw{ri}"])
            for tq in range(4):
                sl = slice(tq * 512, (tq + 1) * 512)
                ti = rot(2)
                P.op("pe", lambda: pe.matmul(p_r[:], RT[:], raw[ri][:, sl], start=True, stop=True),
                     r=["RT", f"raw{ri}"], w=["p_s0"])
                P.op("pool", lambda: pool.tensor_tensor(rt1[ti][:], raw[ri][:, sl], cosT[:, sl], op=ALU.mult),
                     r=[f"raw{ri}", "cosT"], w=[f"rt1_{ti}"])
                P.op("dve", lambda: dve.tensor_tensor(rt2[ti][:], p_r[:], sinT[:, sl], op=ALU.mult),
                     r=["p_s0", "sinT"], w=[f"rt2_{ti}"])
                P.op("dve", lambda: dve.tensor_tensor(dst_fn(sl), rt1[ti][:], rt2[ti][:], op=ALU.add),
                     r=[f"rt1_{ti}", f"rt2_{ti}"], w=[dstk], nowaw=True)

        npt = {"n": 0}
        for g in range(2):
            rope(kT[g], lambda sl: kr[:, sl], "kr")
            for h in range(4):
                rope(qT[g * 4 + h], lambda sl: qr[:, h, sl], "qr")
            if g == 0:
                prefetch_merge_weights(l)
            P.dma("sp", esink[:], sinkx[l, g:g + 1, :].partition_broadcast(128), w=["esink"])
            P.op("act", lambda: act.activation(esink[:], esink[:], AF.Exp), r=["esink"], w=["esink"])
            P.dma("sp", zat[:], zatT[g * 4:(g + 1) * 4].rearrange("c p t -> p c t"), r=["zatT"], w=["zat"])
            def stage1(qb):
                kbs = [kb for kb in (qb - 1, qb, qb + 1) if 0 <= kb < 16]
                qsl = slice(qb * 128, (qb + 1) * 128)
                pts = []
                for i, kb in enumerate(kbs):
                    pti = npt["n"] % 6
                    psi = npt["n"] % 4
                    npt["n"] += 1
                    pts.append(pti)
                    P.op("pe", lambda: pe.matmul(p_s[psi][:].rearrange("p (h q) -> p h q", h=4), kr[:, kb * 128:(kb + 1) * 128],
                                                 qr[:, :, qsl], start=True, stop=True),
                         r=["kr", "qr"], w=[f"p_s{psi}"])
                    P.op("act", lambda: act.activation(PT[pti][:], p_s[psi][:], AF.Exp, scale=float(scale)),
                         r=[f"p_s{psi}"], w=[f"PT{pti}"])
                    if kb != qb:
                        mi = 0 if kb < qb else 1
                        P.op("dve", lambda: dve.tensor_tensor(PT[pti][:], PT[pti][:], masks[:, mi, :], op=ALU.mult),
                             r=[f"PT{pti}", "masks"], w=[f"PT{pti}"])
                return pts

            def stage2(qb, pts):
                qsl = slice(qb * 128, (qb + 1) * 128)
                kbs = [kb for kb in (qb - 1, qb, qb + 1) if 0 <= kb < 16]
                n = len(pts)
                oi = qb % 2
                p_o, p_d = p_o2[oi], p_d2[oi]
                for i in range(n):
                    P.op("pe", lambda: pe.matmul(p_d[:], ones[:], PT[pts[i]][:], start=(i == 0), stop=(i == n - 1)),
                         r=["ones", f"PT{pts[i]}"], w=[f"p_d{oi}"], inc=(i == n - 1))
                for i in range(n):
                    P.op("pe", lambda: pe.matmul(p_o[:], v_at[:, kbs[i], g * 128:(g + 1) * 128],
                                                 PT[pts[i]][:], start=(i == 0), stop=(i == n - 1)),
                         r=["v_at", f"PT{pts[i]}"], w=[f"p_o{oi}"], inc=(i == n - 1))
                P.op("dve", lambda: dve.tensor_tensor(den[oi][:], p_d[:], esink[:], op=ALU.add),
                     r=[f"p_d{oi}", "esink"], w=[f"den{oi}"])
                P.op("dve", lambda: dve.reciprocal(den[oi][:], den[oi][:]), r=[f"den{oi}"], w=[f"den{oi}"])
                P.op("dve", lambda: dve.tensor_tensor(otmp[oi][:], p_o[:], den[oi][:], op=ALU.mult),
                     r=[f"p_o{oi}", f"den{oi}"], w=[f"otmp{oi}"])
                P.op("pool", lambda: pool.tensor_tensor(yat[:, :, qsl], otmp[oi][:].rearrange("p (h q) -> p h q", h=4),
                                                        zat[:, :, qsl], op=ALU.mult),
                     r=[f"otmp{oi}", "zat"], w=["yat"], nowaw=True)

            nxt = stage1(0)
            for qb in range(16):
                cur = nxt
                if qb + 1 < 16:
                    nxt = stage1(qb + 1)
                stage2(qb, cur)
            P.store("sp", yatT[g * 4:(g + 1) * 4].rearrange("c p t -> p c t"), yat[:], r=["yat"], w=["yatT"], sem="yat")
        P.phase_end()

    def prefetch_merge_weights(l):
        whs, was = hold["whs"], hold["was"]
        for q in range(4):
            csl = slice(q * 512, (q + 1) * 512)
            P.dma("pool", whs[:, :, csl], w_hy[l].rearrange("(c p) n -> p c n", p=128)[:, :, csl], w=[f"whs{q}"])
            P.dma("pool", was[:, :, csl], w_at[l].rearrange("(c p) n -> p c n", p=128)[:, :, csl], w=[f"was{q}"])

    def phase_merge(l):
        P.phase_begin()
        whs, was = hold["whs"], hold["was"]
        yh = P.sb("yh", [128, 8, L], BF16)
        ya = P.sb("ya", [128, 8, L], BF16)
        for tq in range(4):
            tsl = slice(tq * 512, (tq + 1) * 512)
            P.dma("sp", yh[:, :, tsl], yhyT.rearrange("c p t -> p c t")[:, :, tsl], r=["yhyT"], w=[f"yh{tq}"])
            P.dma("sp", ya[:, :, tsl], yatT.rearrange("c p t -> p c t")[:, :, tsl], r=["yatT"], w=[f"ya{tq}"])
        sgh = [P.sb(f"sgh{i}", [128, L], BF16) for i in range(2)]
        sga = [P.sb(f"sga{i}", [128, L], BF16) for i in range(2)]
        mrow = [P.sb(f"mrow{i}", [128, L], BF16) for i in range(2)]
        m1 = [P.sb(f"m1_{i}", [128, 512], F32) for i in range(2)]
        m2 = [P.sb(f"m2_{i}", [128, 512], F32) for i in range(2)]
        pa = [P.ps(f"pa{i}", [128, 512], F32) for i in range(2)]
        pb = [P.ps(f"pb{i}", [128, 512], F32) for i in range(2)]
        for Dc in range(16):
            si = Dc % 2
            P.dma("sp", sgh[si][:], sgT[Dc], r=["sgT"], w=[f"sgh{si}"])
            P.dma("sp", sga[si][:], sgT[16 + Dc], r=["sgT"], w=[f"sga{si}"])
            for tq in range(4):
                tsl = slice(tq * 512, (tq + 1) * 512)
                pi = rot(2)
                for cc in range(8):
                    P.op("pe", lambda: pe.matmul(pa[pi][:], whs[:, cc, Dc * 128:(Dc + 1) * 128], yh[:, cc, tsl],
                                                 start=(cc == 0), stop=(cc == 7)),
                         r=[f"whs{Dc // 4}", f"yh{tq}"], w=[f"pa{pi}"], inc=(cc == 7))
                for cc in range(8):
                    P.op("pe", lambda: pe.matmul(pb[pi][:], was[:, cc, Dc * 128:(Dc + 1) * 128], ya[:, cc, tsl],
                                                 start=(cc == 0), stop=(cc == 7)),
                         r=[f"was{Dc // 4}", f"ya{tq}"], w=[f"pb{pi}"], inc=(cc == 7))
                P.op("dve", lambda: dve.tensor_tensor(m1[pi][:], pa[pi][:], sgh[si][:, tsl], op=ALU.mult),
                     r=[f"pa{pi}", f"sgh{si}"], w=[f"m1_{pi}"])
                P.op("dve", lambda: dve.tensor_tensor(m2[pi][:], pb[pi][:], sga[si][:, tsl], op=ALU.mult),
                     r=[f"pb{pi}", f"sga{si}"], w=[f"m2_{pi}"])
                P.op("pool", lambda: pool.tensor_tensor(mrow[si][:, tsl], m1[pi][:], m2[pi][:], op=ALU.add),
                     r=[f"m1_{pi}", f"m2_{pi}"], w=[f"mrow{si}"], nowaw=True)
            P.store("sp", mrgT[Dc], mrow[si][:], r=[f"mrow{si}"], w=["mrgT"], sem=f"mrow{si}")
        P.phase_end()

    def phase_out(l, xsrc, xdst, final):
        P.phase_begin()
        wo = P.sb("wo", [128, 16, D], BF16)
        wv = w_out[l].rearrange("(c p) n -> p c n", p=128)
        for qd in range(4):
            csl = slice(qd * 512, (qd + 1) * 512)
            P.dma("pool", wo[:, :, csl], wv[:, :, csl], w=[f"wo{qd}"])
        mt = [P.sb(f"mt{i}", [128, 16, 512], BF16) for i in range(2)]
        xt = [P.sb(f"oxt{i}", [128, D], F32) for i in range(2)]
        xo = [P.sb(f"oxo{i}", [128, D], F32) for i in range(2)]
        po = [P.ps(f"po{i}", [128, 512], F32) for i in range(4)]
        if final:
            g_bc = P.sb("gf_bc", [128, D], F32)
            P.dma("sp", g_bc[:], final_norm[0:1, :].partition_broadcast(128), w=["gf_bc"])
            sq = P.sb("osq", [128, D], F32)
            ss = [P.sb(f"oss{i}", [128, 1], F32) for i in range(2)]
            yo = [P.sb(f"oyo{i}", [128, D], F32) for i in range(2)]
        for t4 in range(4):
            mi = t4 % 2
            P.dma("sp", mt[mi][:], mrgT[:, :, t4 * 512:(t4 + 1) * 512].rearrange("c p t -> p c t"), r=["mrgT"], w=[f"mt{mi}"])
            for tt in range(4):
                tb = t4 * 4 + tt
                i = tb % 2
                P.dma("sp", xt[i][:], xsrc[tb * 128:(tb + 1) * 128, :], w=[f"oxt{i}"])
                for nq in range(4):
                    pi = rot(4)
                    nsl = slice(nq * 512, (nq + 1) * 512)
                    for Dc in range(16):
                        P.op("pe", lambda: pe.matmul(po[pi][:], mt[mi][:, Dc, tt * 128:(tt + 1) * 128], wo[:, Dc, nsl],
                                                     start=(Dc == 0), stop=(Dc == 15)),
                             r=[f"mt{mi}", f"wo{nq}"], w=[f"po{pi}"], inc=(Dc == 15))
                    P.op("dve", lambda: dve.tensor_tensor(xo[i][:, nsl], po[pi][:], xt[i][:, nsl], op=ALU.add),
                         r=[f"po{pi}", f"oxt{i}"], w=[f"oxo{i}"], nowaw=True)
                if not final:
                    P.store("sp", xdst[tb * 128:(tb + 1) * 128, :], xo[i][:], r=[f"oxo{i}"], w=["xdst"], sem=f"oxo{i}")
                else:
                    P.op("act", lambda: act.activation(sq[:], xo[i][:], AF.Square, accum_out=ss[i][:]),
                         r=[f"oxo{i}"], w=["osq", f"oss{i}"])
                    P.op("dve", lambda: dve.tensor_scalar(ss[i][:], ss[i][:], 1.0 / D, 1e-6, op0=ALU.mult, op1=ALU.add),
                         r=[f"oss{i}"], w=[f"oss{i}"])
                    P.op("act", lambda: act.sqrt(ss[i][:], ss[i][:]), r=[f"oss{i}"], w=[f"oss{i}"])
                    P.op("dve", lambda: dve.reciprocal(ss[i][:], ss[i][:]), r=[f"oss{i}"], w=[f"oss{i}"])
                    P.op("dve", lambda: dve.scalar_tensor_tensor(yo[i][:], xo[i][:], ss[i][:, 0:1], g_bc[:],
                                                                 op0=ALU.mult, op1=ALU.mult),
                         r=[f"oxo{i}", f"oss{i}", "gf_bc"], w=[f"oyo{i}"])
                    P.store("sp", xdst[tb * 128:(tb + 1) * 128, :], yo[i][:], r=[f"oyo{i}"], w=["xdst"], sem=f"oyo{i}")
        P.phase_end()

    stop = [d for d in dbg if d.startswith("stop_")]
    stop = stop[0] if stop else None
    xsrc = x_in
    for l in range(nlayers):
        last = (l == nlayers - 1)
        phase_filter(l)
        if stop == "stop_filter":
            break
        P.phase_begin()
        hold["v_at"] = P.sb("v_at", [128, 16, 256], BF16)
        P.phase_begin()
        hold["vx_tok"] = P.sb("vx_tok", [128, 2, 16, C], BF16)
        phase_proj(l, xsrc)
        if stop == "stop_proj":
            break
        phase_hyena(l)
        P.phase_end()
        if stop == "stop_hyena":
            break
        P.phase_begin()
        hold["whs"] = P.sb("whs", [128, 8, D], BF16)
        hold["was"] = P.sb("was", [128, 8, D], BF16)
        phase_attn(l)
        if stop == "stop_attn":
            break
        phase_merge(l)
        P.phase_end()
        P.phase_end()
        if stop == "stop_merge":
            break
        phase_out(l, xsrc, out if last else xa, last)
        xsrc = xa
    P.finish()
    return nc, P


_SMALL = ("norm_g", "filt_w1", "filt_w2", "filt_w3", "filt_w4", "w_in", "w_hyena_out", "w_attn_out", "w_out")


def host_inputs(inputs):
    f32 = np.float32
    c = make_consts()
    shared = {}
    for k in _SMALL:
        shared[k] = np.ascontiguousarray(inputs[k], dtype=f32)
    shared["final_norm"] = np.ascontiguousarray(inputs["final_norm"], dtype=f32).reshape(1, D)
    cw = np.asarray(inputs["conv_w"], dtype=f32)
    shared["conv_wp"] = np.ascontiguousarray(cw.reshape(2, 3, 24, 128).transpose(0, 3, 1, 2))
    cb = np.asarray(inputs["conv_b"], dtype=f32)
    shared["conv_b"] = np.ascontiguousarray(cb)
    shared["conv_bp"] = np.ascontiguousarray(cb.reshape(2, 24, 128).transpose(0, 2, 1))
    shared["filt_sc"] = np.ascontiguousarray(np.stack(
        [np.asarray(inputs[k], dtype=f32) for k in ("filt_freq", "filt_b1", "filt_b2", "filt_b3")], axis=-1))
    shared["hyena_bias"] = np.ascontiguousarray(np.asarray(inputs["hyena_bias"], dtype=f32).reshape(2, 2 * C))
    sk = np.asarray(inputs["attn_sink"], dtype=f32)
    shared["sinkx"] = np.ascontiguousarray(np.repeat(sk.reshape(2, 2, 4, 1), 128, axis=3).reshape(2, 2, 512))
    for k in ("FT", "FTP", "GT", "featsT", "deltas", "negt", "ropeC", "ropeS", "RT", "masks"):
        shared[k] = c[k]
    x = np.asarray(inputs["x"], dtype=f32)
    return [dict(shared, x=np.ascontiguousarray(x[b])) for b in range(8)]


_NC = None


def kernel(**inputs):
    global _NC
    if _NC is None:
        _NC = build()[0]
    in_maps = host_inputs(inputs)
    res = run_bass_kernel_spmd(_NC, in_maps, core_ids=list(range(8)))
    return np.stack([np.asarray(r["out"], dtype=np.float32) for r in res.results], axis=0)
```
